# Optimizing a Trainium2 kernel written in Bass

```python
import jax, jax.numpy as jnp
from jax import lax
import numpy as np

D_MODEL = 1024
BATCH = 32
SEQ = 256
DEPTH = 4
DEC_BATCH = 2
DEC_SEQ = 4096
PAST_LEN = 256

GRID_W = 64
HEAD_DIM = 64
NA_HEADS = 6
NA_KH = 8
NA_KW = 16
MLA_HEADS = 4
MLA_Q_LORA = 256
MLA_KV_LORA = 128
MLA_NOPE = 64
MLA_ROPE = 32
MLA_V = 64
GQA_HEADS = 6
GQA_KV_HEADS = 2
GQA_GROUP = GQA_HEADS // GQA_KV_HEADS
D_FF = 2816
CONV_W = 3
Q_BLOCK = 128
ROPE_THETA = 10000.0
EPS = 1e-6

NA_WIDTH = NA_HEADS * HEAD_DIM
MLA_WIDTH = MLA_HEADS * MLA_V
GQA_WIDTH = GQA_HEADS * HEAD_DIM
MIX_WIDTH = NA_WIDTH + MLA_WIDTH + GQA_WIDTH
IN_SPLITS = (NA_WIDTH, NA_WIDTH, NA_WIDTH,
             MLA_Q_LORA, MLA_KV_LORA, MLA_ROPE,
             GQA_WIDTH, GQA_KV_HEADS * HEAD_DIM, GQA_KV_HEADS * HEAD_DIM)
IN_COLS = sum(IN_SPLITS)

kernel_name = 'hybrid_na_mla_gqa_diffusion_step'


def rms_norm(x, g):
    xf = x.astype(jnp.float32)
    y = xf * lax.rsqrt(jnp.mean(xf * xf, axis=-1, keepdims=True) + EPS)
    return y.astype(x.dtype) * g


def axial_rope(x):
    n, d = x.shape[1], x.shape[-1]
    half = d // 2
    quarter = half // 2
    t = jnp.arange(n, dtype=jnp.int32)
    freqs = ROPE_THETA ** (-jnp.arange(quarter, dtype=jnp.float32) / quarter)
    outs = []
    for pos, xs in ((t // GRID_W, x[..., :half]), (t % GRID_W, x[..., half:])):
        ang = pos.astype(jnp.float32)[:, None] * freqs[None, :]
        cos = jnp.cos(ang)[None, :, None, :].astype(x.dtype)
        sin = jnp.sin(ang)[None, :, None, :].astype(x.dtype)
        x1, x2 = xs[..., :quarter], xs[..., quarter:]
        outs.append(jnp.concatenate([x1 * cos - x2 * sin, x2 * cos + x1 * sin], axis=-1))
    return jnp.concatenate(outs, axis=-1)


def blocked_attention(q, k, v):
    b, lq, hkv, g, dk = q.shape
    dv = v.shape[-1]
    scale = dk ** -0.5
    qb = jnp.moveaxis(q.reshape(b, lq // Q_BLOCK, Q_BLOCK, hkv, g, dk), 1, 0)

    def one_block(q_blk):
        s = jnp.einsum('bqhgd,bkhd->bhgqk', q_blk, k).astype(jnp.float32) * scale
        p = jax.nn.softmax(s, axis=-1).astype(v.dtype)
        return jnp.einsum('bhgqk,bkhd->bqhgd', p, v)

    o = lax.map(one_block, qb)
    return jnp.moveaxis(o, 0, 1).reshape(b, lq, hkv * g * dv)


def neighbourhood_attention(q, k, v, k_ctx, v_ctx, rpb):
    b, n, h, d = q.shape
    rows = n // GRID_W
    kh = min(NA_KH, rows)
    scale = d ** -0.5
    cols = np.arange(GRID_W)
    col_start = np.clip(cols - NA_KW // 2, 0, GRID_W - NA_KW)
    kw_idx = col_start[:, None] + np.arange(NA_KW)[None, :]
    dc_idx = kw_idx - cols[:, None] + (NA_KW - 1)
    col_bias = rpb[:, :, dc_idx]
    qg = jnp.moveaxis(q.reshape(b, rows, GRID_W, h, d), 1, 0)
    kg = k.reshape(b, rows, GRID_W, h, d)
    vg = v.reshape(b, rows, GRID_W, h, d)

    def one_row(args):
        r, q_row = args
        rs = jnp.clip(r - kh // 2, 0, rows - kh)
        k_blk = lax.dynamic_slice_in_dim(kg, rs, kh, axis=1)[:, :, kw_idx]
        v_blk = lax.dynamic_slice_in_dim(vg, rs, kh, axis=1)[:, :, kw_idx]
        bias = jnp.take(col_bias, rs - r + jnp.arange(kh) + (NA_KH - 1), axis=1)
        s_nb = (jnp.einsum('bqhd,bjqkhd->bhqjk', q_row, k_blk).astype(jnp.float32) * scale
                + jnp.transpose(bias, (0, 2, 1, 3))[None].astype(jnp.float32))
        s_cx = jnp.einsum('bqhd,bchd->bhqc', q_row, k_ctx).astype(jnp.float32) * scale
        s = jnp.concatenate([s_nb.reshape(b, h, GRID_W, kh * NA_KW), s_cx], axis=-1)
        p = jax.nn.softmax(s, axis=-1).astype(v.dtype)
        p_nb = p[..., :kh * NA_KW].reshape(b, h, GRID_W, kh, NA_KW)
        return (jnp.einsum('bhqjk,bjqkhd->bqhd', p_nb, v_blk)
                + jnp.einsum('bhqc,bchd->bqhd', p[..., kh * NA_KW:], v_ctx))

    o = lax.map(one_row, (jnp.arange(rows, dtype=jnp.int32), qg))
    return jnp.moveaxis(o, 0, 1).reshape(b, n, h * d)


def mixer_projections(h, w_in, mla_g_q, mla_w_uq, mla_g_kv, gqa_g_q, gqa_g_k):
    b, n, _ = h.shape
    idx = [int(i) for i in np.cumsum(IN_SPLITS)[:-1]]
    q_na, k_na, v_na, cq, ckv, k_rope, q_g, k_g, v_g = jnp.split(h @ w_in, idx, axis=-1)
    q_na = q_na.reshape(b, n, NA_HEADS, HEAD_DIM)
    k_na = k_na.reshape(b, n, NA_HEADS, HEAD_DIM)
    v_na = v_na.reshape(b, n, NA_HEADS, HEAD_DIM)
    q_mla = (rms_norm(cq, mla_g_q) @ mla_w_uq).reshape(b, n, MLA_HEADS, MLA_NOPE + MLA_ROPE)
    ckv = rms_norm(ckv, mla_g_kv)
    q_g = rms_norm(q_g.reshape(b, n, GQA_HEADS, HEAD_DIM), gqa_g_q)
    k_g = rms_norm(k_g.reshape(b, n, GQA_KV_HEADS, HEAD_DIM), gqa_g_k)
    v_g = v_g.reshape(b, n, GQA_KV_HEADS, HEAD_DIM)
    return q_na, k_na, v_na, q_mla, ckv, k_rope, q_g, k_g, v_g


def mla_expand(ckv, k_rope, w_ukv):
    b, l, _ = ckv.shape
    kv = (ckv @ w_ukv).reshape(b, l, MLA_HEADS, MLA_NOPE + MLA_V)
    k_nope, v = kv[..., :MLA_NOPE], kv[..., MLA_NOPE:]
    k = jnp.concatenate([k_nope, jnp.broadcast_to(k_rope[:, :, None, :], (b, l, MLA_HEADS, MLA_ROPE))], axis=-1)
    return k, v


def conv_ffn(h, w_up, conv_w, conv_b, w_down):
    u = h @ w_up
    up = jnp.pad(u, ((0, 0), (1, 1), (0, 0)))
    u = up[:, :-2] * conv_w[0] + up[:, 1:-1] * conv_w[1] + up[:, 2:] * conv_w[2] + conv_b
    gate, val = jnp.split(u, 2, axis=-1)
    return (jax.nn.silu(gate) * val) @ w_down


def layer(x, cond, w, ctx=None):
    (w_ada, b_ada, g_pre_mix, g_post_mix, g_pre_ffn, g_post_ffn, w_in, na_rpb,
     mla_g_q, mla_w_uq, mla_g_kv, mla_w_ukv, gqa_g_q, gqa_g_k, w_out,
     ffn_w_up, ffn_conv_w, ffn_conv_b, ffn_w_down) = w
    b, n, _ = x.shape
    mods = jax.nn.silu(cond) @ w_ada + b_ada
    sh_m, sc_m, gt_m, sh_f, sc_f, gt_f = [m[:, None, :] for m in jnp.split(mods, 6, axis=-1)]
    h = rms_norm(x, g_pre_mix) * (1 + sc_m) + sh_m
    q_na, k_na, v_na, q_mla, ckv, k_rope, q_g, k_g, v_g = mixer_projections(
        h, w_in, mla_g_q, mla_w_uq, mla_g_kv, gqa_g_q, gqa_g_k)
    if ctx is None:
        o_na = blocked_attention(q_na[:, :, :, None], k_na, v_na)
        k_m, v_m = mla_expand(ckv, k_rope, mla_w_ukv)
        o_mla = blocked_attention(q_mla[:, :, :, None], k_m, v_m)
        o_gqa = blocked_attention(q_g.reshape(b, n, GQA_KV_HEADS, GQA_GROUP, HEAD_DIM), k_g, v_g)
        new_ctx = (k_na, v_na, ckv, k_rope, k_g, v_g)
    else:
        ctx_na_k, ctx_na_v, ctx_ckv, ctx_krope, ctx_gqa_k, ctx_gqa_v = ctx
        o_na = neighbourhood_attention(q_na, k_na, v_na, ctx_na_k, ctx_na_v, na_rpb)
        q_mla = jnp.concatenate([q_mla[..., :MLA_NOPE], axial_rope(q_mla[..., MLA_NOPE:])], axis=-1)
        k_lat, v_lat = mla_expand(ckv, axial_rope(k_rope[:, :, None, :])[:, :, 0, :], mla_w_ukv)
        k_cx, v_cx = mla_expand(ctx_ckv, ctx_krope, mla_w_ukv)
        o_mla = blocked_attention(q_mla[:, :, :, None],
                                  jnp.concatenate([k_cx, k_lat], axis=1),
                                  jnp.concatenate([v_cx, v_lat], axis=1))
        q_r = axial_rope(q_g).reshape(b, n, GQA_KV_HEADS, GQA_GROUP, HEAD_DIM)
        o_gqa = blocked_attention(q_r,
                                  jnp.concatenate([ctx_gqa_k, axial_rope(k_g)], axis=1),
                                  jnp.concatenate([ctx_gqa_v, v_g], axis=1))
        new_ctx = None
    o = jnp.concatenate([o_na, o_mla, o_gqa], axis=-1) @ w_out
    x = x + gt_m * rms_norm(o, g_post_mix)
    h = rms_norm(x, g_pre_ffn) * (1 + sc_f) + sh_f
    x = x + gt_f * rms_norm(conv_ffn(h, ffn_w_up, ffn_conv_w, ffn_conv_b, ffn_w_down), g_post_ffn)
    return x, new_ctx


def setup_inputs(seed: int = 0) -> dict:
    key = jax.random.key(seed)
    sub = jax.random.split(key, 40)
    ks = iter([sub[i] for i in range(40)])

    def nrm(shape, s):
        return jax.random.normal(next(ks), shape, jnp.float32) * s

    def gain(shape):
        return 1.0 + nrm(shape, 0.05)

    L, D = DEPTH, D_MODEL
    return {
        'x_prompt': nrm((BATCH, SEQ, D), 1.0),
        'x_sample': nrm((DEC_BATCH, DEC_SEQ, D), 1.0),
        'c': nrm((DEC_BATCH, D), 1.0),
        'cache_na_k': nrm((DEC_BATCH, L, PAST_LEN, NA_HEADS, HEAD_DIM), 1.0),
        'cache_na_v': nrm((DEC_BATCH, L, PAST_LEN, NA_HEADS, HEAD_DIM), 1.0),
        'cache_mla_ckv': nrm((DEC_BATCH, L, PAST_LEN, MLA_KV_LORA), 1.0),
        'cache_mla_krope': nrm((DEC_BATCH, L, PAST_LEN, MLA_ROPE), 1.0),
        'cache_gqa_k': nrm((DEC_BATCH, L, PAST_LEN, GQA_KV_HEADS, HEAD_DIM), 1.0),
        'cache_gqa_v': nrm((DEC_BATCH, L, PAST_LEN, GQA_KV_HEADS, HEAD_DIM), 1.0),
        'c_ctx': nrm((D,), 1.0),
        'w_ada': nrm((L, D, 6 * D), 0.5 * D ** -0.5),
        'b_ada': nrm((L, 6 * D), 0.02),
        'g_pre_mix': gain((L, D)),
        'g_post_mix': gain((L, D)),
        'g_pre_ffn': gain((L, D)),
        'g_post_ffn': gain((L, D)),
        'w_in': nrm((L, D, IN_COLS), D ** -0.5),
        'na_rpb': nrm((L, NA_HEADS, 2 * NA_KH - 1, 2 * NA_KW - 1), 0.1),
        'mla_g_q': gain((L, MLA_Q_LORA)),
        'mla_w_uq': nrm((L, MLA_Q_LORA, MLA_HEADS * (MLA_NOPE + MLA_ROPE)), MLA_Q_LORA ** -0.5),
        'mla_g_kv': gain((L, MLA_KV_LORA)),
        'mla_w_ukv': nrm((L, MLA_KV_LORA, MLA_HEADS * (MLA_NOPE + MLA_V)), MLA_KV_LORA ** -0.5),
        'gqa_g_q': gain((L, HEAD_DIM)),
        'gqa_g_k': gain((L, HEAD_DIM)),
        'w_out': nrm((L, MIX_WIDTH, D), MIX_WIDTH ** -0.5),
        'ffn_w_up': nrm((L, D, 2 * D_FF), D ** -0.5),
        'ffn_conv_w': nrm((L, CONV_W, 2 * D_FF), CONV_W ** -0.5),
        'ffn_conv_b': nrm((L, 2 * D_FF), 0.02),
        'ffn_w_down': nrm((L, D_FF, D), D_FF ** -0.5),
    }


def reference(x_prompt, x_sample, c, cache_na_k, cache_na_v, cache_mla_ckv, cache_mla_krope,
              cache_gqa_k, cache_gqa_v, c_ctx, w_ada, b_ada, g_pre_mix, g_post_mix, g_pre_ffn,
              g_post_ffn, w_in, na_rpb, mla_g_q, mla_w_uq, mla_g_kv, mla_w_ukv, gqa_g_q, gqa_g_k,
              w_out, ffn_w_up, ffn_conv_w, ffn_conv_b, ffn_w_down):
    cond_ctx = c_ctx[None, :]
    xp, xs = x_prompt, x_sample
    na_k, na_v, mla_ckv, mla_krope, gqa_k, gqa_v = [], [], [], [], [], []
    for l in range(DEPTH):
        w = (w_ada[l], b_ada[l], g_pre_mix[l], g_post_mix[l], g_pre_ffn[l], g_post_ffn[l],
             w_in[l], na_rpb[l], mla_g_q[l], mla_w_uq[l], mla_g_kv[l], mla_w_ukv[l],
             gqa_g_q[l], gqa_g_k[l], w_out[l], ffn_w_up[l], ffn_conv_w[l], ffn_conv_b[l],
             ffn_w_down[l])
        xp, (k1, v1, ck, kr, k2, v2) = layer(xp, cond_ctx, w)
        na_k.append(k1)
        na_v.append(v1)
        mla_ckv.append(ck)
        mla_krope.append(kr)
        gqa_k.append(k2)
        gqa_v.append(v2)
        ctx_l = (cache_na_k[:, l], cache_na_v[:, l], cache_mla_ckv[:, l], cache_mla_krope[:, l],
                 cache_gqa_k[:, l], cache_gqa_v[:, l])
        xs, _ = layer(xs, c, w, ctx_l)
    y_prompt, y_sample = xp, xs
    new_na_k = jnp.stack(na_k, axis=1)
    new_na_v = jnp.stack(na_v, axis=1)
    new_mla_ckv = jnp.stack(mla_ckv, axis=1)
    new_mla_krope = jnp.stack(mla_krope, axis=1)
    new_gqa_k = jnp.stack(gqa_k, axis=1)
    new_gqa_v = jnp.stack(gqa_v, axis=1)
    return (y_prompt, y_sample, new_na_k, new_na_v, new_mla_ckv, new_mla_krope, new_gqa_k, new_gqa_v)
```

```python
import contextlib
import numpy as np
import concourse.bass as bass
import concourse.mybir as mybir
from concourse.bass_utils import run_bass_kernel_spmd

F32 = mybir.dt.float32
BF16 = mybir.dt.bfloat16
ALU = mybir.AluOpType
AF = mybir.ActivationFunctionType

L = 4
D = 1024
NT = 1024
R = 1184
NPL = 263
EPS = 1e-6
import os
NLAYERS_RUN = int(os.environ.get('K_LAYERS', L))
NGROUPS_RUN = int(os.environ.get('K_GROUPS', 2))
STOP_AFTER = os.environ.get('K_STOP', '')
BISECT = int(os.environ.get('K_BISECT', 0))
MAXOPS = int(os.environ.get('K_MAXOPS', 10**9))


class _Rec:
    def __getattr__(self, meth):
        def mk(*a, **k):
            f = lambda e: getattr(e, meth)(*a, **k)
            f._desc = (meth, a, k)
            return f
        return mk


REC = _Rec()


class Prog:
    def __init__(self):
        self.ops = []
        self.lastw = {}
        self.rds = {}
        self.keycnt = {}
        self.pend = {}
        self.last_on = {}
        self.dma_since = []

    def add(self, eng, fn, r=(), w=(), key=None, inc=16, multi=False):
        i = len(self.ops)
        if eng in ("act", "dve"):
            extra = ["rd_" + x for x in r if x.startswith("ps") and len(x) == 3]
            if extra:
                w = list(w) + extra
        deps = set()
        for x in r:
            if x in self.lastw:
                deps.add(self.lastw[x])
        for x in w:
            if x in self.lastw:
                deps.add(self.lastw[x])
            rd = self.rds.get(x)
            if rd:
                deps.update(rd[0].values())
                deps.update(rd[1])
        if eng in self.pend:
            deps.update(self.pend.pop(eng))
        isdma = key is not None
        for x in r:
            rd = self.rds.setdefault(x, [{}, []])
            if isdma:
                rd[1].append(i)
            else:
                rd[0][eng] = i
        for x in w:
            self.lastw[x] = i
            self.rds[x] = [{}, []]
        val = None
        if isdma:
            self.keycnt[key] = self.keycnt.get(key, 0) + inc
            val = self.keycnt[key]
            self.dma_since.append(i)
        else:
            self.last_on[eng] = i
        self.ops.append(dict(eng=eng, fn=fn, deps=deps, key=key, inc=inc, val=val, sig=isdma, multi=multi))
        return i

    def barrier(self):
        b = set(self.last_on.values()) | set(self.dma_since)
        self.dma_since = []
        for e in ("pe", "act", "dve", "pool", "sp"):
            self.pend[e] = set(b) | self.pend.get(e, set())

    def finalize(self):
        ops = self.ops
        for o in ops:
            keep = set()
            for d in o["deps"]:
                if ops[d]["eng"] == "pe" and o["eng"] == "pe" and ops[d]["key"] is None:
                    continue
                ops[d]["sig"] = True
                keep.add(d)
            o["deps"] = keep
        cnt = {}
        for o in ops:
            if o["key"] is None and o["sig"]:
                cnt[o["eng"]] = cnt.get(o["eng"], 0) + 1
                o["val"] = cnt[o["eng"]]

    def run(self, eng, h, sems):
        ops = self.ops
        waited = {}
        for oi, o in enumerate(ops):
            if o["eng"] != eng:
                continue
            if oi >= MAXOPS and oi != len(ops) - 1:
                continue
            need = {}
            dl = o["deps"]
            if oi == len(ops) - 1 and MAXOPS < len(ops):
                dl = [j for j in range(min(MAXOPS, len(ops) - 1)) if ops[j]["key"] is not None]
            for d in dl:
                if d >= MAXOPS:
                    continue
                dd = ops[d]
                s = dd["key"] if dd["key"] is not None else "E:" + dd["eng"]
                if dd["val"] > need.get(s, 0):
                    need[s] = dd["val"]
            for s, v in need.items():
                if waited.get(s, 0) < v:
                    h.wait_ge(sems[s], v)
                    waited[s] = v
            if o["multi"]:
                o["fn"](h, sems[o["key"]])
                continue
            ins = o["fn"](h)
            if o["sig"]:
                if o["key"] is not None:
                    ins.then_inc(sems[o["key"]], o["inc"])
                else:
                    ins.then_inc(sems["E:" + eng], 1)


def build():
    nc = bass.Bass("TRN2", target_bir_lowering=False)
    P = Prog()

    def din(name, shape, dt=F32):
        return nc.dram_tensor(name, shape, dt, kind="ExternalInput").ap()

    xT_d = din("xT", [D, 2 * NT])
    cond_d = din("condT", [D, 2])
    ctxk_d = din("ctxk", [L, 672, 256])
    ctxv_d = din("ctxv", [L, 256, 512])
    wada_d = din("w_ada", [NLAYERS_RUN, D, 6 * D] if not BISECT else [1, 128, 128])
    win_d = din("w_in", [NLAYERS_RUN, D, 2208])
    winsw_d = din("w_insw", [NLAYERS_RUN, D, 544])
    wuq_d = din("w_uq", [L, 256, 384])
    wuqsw_d = din("w_uqsw", [L, 256, 384])
    wukv_d = din("w_ukv", [L, 128, 512])
    wout_d = din("w_out", [NLAYERS_RUN, D, D] if not BISECT else [1, 128, 128])
    wup_d = din("w_up", [NLAYERS_RUN, D, 5632] if not BISECT else [1, 128, 128])
    wdn_d = din("w_down", [NLAYERS_RUN, 2816, D] if not BISECT else [1, 128, 128])
    pp_d = din("pp", [128, L * NPL])
    tm_d = din("tm", [128, L * 192])
    rope_d = din("rope", [128, 4, NT])
    dsh_d = din("dsh", [31, 127])
    colmask_d = din("colmask", [128, 64])
    rowsel_d = din("rowsel", [2, 128])
    maskrow_d = din("maskrow", [2, 192])
    rpbr_d = din("rpbr", [L, 31, 90])
    yT_d = nc.dram_tensor("yT", [D, 2 * NT], F32, kind="ExternalOutput").ap()
    cache_d = nc.dram_tensor("caches", [L, NT, R], F32, kind="ExternalOutput").ap()
    CH = [(0, 256), (256, 256), (512, 160), (672, 256), (928, 256)]
    sendc = [nc.dram_tensor("send%d" % i, [n, NT], BF16).ap() for i, (o_, n) in enumerate(CH)]
    recvc = [nc.dram_tensor("recv%d" % i, [6 * n, NT], BF16).ap() for i, (o_, n) in enumerate(CH)]
    nawins = {i: nc.dram_tensor("nawin%d" % i, [3 * 256, NT], BF16).ap() for i in (0, 1, 3, 4)}
    RECV_ALL = ["recv%d" % i for i in range(5)]

    def chunk_of(a, b):
        for i, (o_, n) in enumerate(CH):
            if a >= o_ and b <= o_ + n:
                return i, a - o_, b - o_
        raise AssertionError((a, b))

    def send_rows(a, b):
        i, la, lb = chunk_of(a, b)
        return sendc[i][la:lb, :], "send%d" % i

    def sendTM_tile(tile):
        i, la, lb = chunk_of(672 + tile * 64, 672 + tile * 64 + 64)
        return sendc[i][la:lb, :].rearrange("r (a c) -> (r a) c", c=512), "send%d" % i

    def recvF_rows(a, b):
        i, la, lb = chunk_of(a, b)
        n = CH[i][1]
        return recvc[i][n:5 * n, :].rearrange("(r f) t -> f r t", f=n)[la:lb]

    def recvT_rank_half(r_, half):
        return recvc[3 + half][(1 + r_) * 256:(2 + r_) * 256, :].rearrange("f (a c) -> (f a) c", c=512)

    def nawinF(k, a, b):
        i, la, lb = chunk_of(a, b)
        return nawins[i][k * 256 + la:k * 256 + lb, :]

    def nawinT(k, half):
        return nawins[3 + half][k * 256:(k + 1) * 256, :].rearrange("f (a c) -> (f a) c", c=512)
    hsend_t = nc.dram_tensor("hsend", [2, D], BF16)
    hrecv_t = nc.dram_tensor("hrecv", [12, D], BF16)
    hsend = hsend_t.ap()
    hrecv = hrecv_t.ap()

    es = contextlib.ExitStack()
    with es:
        def sb(name, shape, dt):
            return es.enter_context(nc.sbuf_tensor("sb_" + name, shape, dt))

        def ps(name):
            return es.enter_context(nc.psum_tensor(name, [128, 1024], F32))

        PS = [ps("ps%d" % i) for i in range(4)]

        def bank(i, n=512, p0=0, p1=128):
            return PS[i // 2][p0:p1, (i % 2) * 512:(i % 2) * 512 + n]

        def bn(i):
            return "ps%d" % i

        x = sb("x", [128, 8, NT], F32)
        hy = sb("hy", [128, 8, NT], BF16)
        WS = [sb("ws%d" % i, [128, 3328], BF16) for i in range(3)]
        WA = [sb("wa%d" % i, [128, 8, 128], F32) for i in range(2)]
        wsm = sb("wsm", [128, 2, 2, 384], BF16)
        wukv = sb("wukv", [128, 512], BF16)
        mods = sb("mods", [128, L, 48, 2], F32)
        pp = sb("pp", [128, L * NPL], F32)
        tmr = sb("tmr", [128, L * 192], F32)
        vec = sb("vec", [128, 4, 8], F32)
        condT = sb("condT", [128, 8, 2], F32)
        ones = sb("ones", [128, 128], BF16)
        blk = sb("blk", [128, 128], BF16)
        rope = sb("rope", [128, 4, NT], BF16)
        dsh = sb("dsh", [31, 127], F32)
        colmask = sb("colmask", [128, 64], BF16)
        rowsel = sb("rowsel", [2, 128], BF16)
        maskrow = sb("maskrow", [2, 192], BF16)
        rpbr = sb("rpbr", [31, 90], F32)
        zero = sb("zero", [128, 1024], BF16)
        S = sb("S", [128, 5632], F32)
        QO = sb("QO", [128, 12, NT], BF16)
        KV = sb("KV", [128, 27648], BF16)

        def sv(off, n, dt=F32, p0=0, p1=128):
            a = S[p0:p1, off:off + n]
            return a if dt == F32 else a.bitcast(dt)

        def kv(off, n, p0=0, p1=128):
            return KV[p0:p1, off:off + n]

        def dma(eng, out, in_, r, w, key, slow=False):
            def f(e, out=out, in_=in_):
                if slow:
                    return e.dma_start(out=out, in_=in_, allow_slow_non_contiguous=True)
                return e.dma_start(out=out, in_=in_)
            return P.add(eng, f, r=r, w=w, key=key)

        stores = []
        RK = {}

        def get_rk(e):
            if "v" not in RK:
                RK["v"] = e.snap(e.partition_id() % 4)
            return RK["v"]

        wq = dict(i=0, issued=0, plan=[])

        def w_issue_upto(n):
            while wq["issued"] < min(n, len(wq["plan"])):
                k = wq["issued"]
                src, shape = wq["plan"][k]
                slot = k % 3
                nel = int(np.prod(shape))
                dst = WS[slot][:, 0:nel]
                if len(shape) == 2:
                    dst = dst.rearrange("p (a b) -> p a b", b=shape[1])
                    dma("pool", dst, src, r=[], w=["ws%d" % slot], key="ws%d" % slot)
                else:
                    dst = dst.rearrange("p (a b c) -> p a b c", b=shape[1], c=shape[2])

                    def f2(e, sem, dst=dst, src=src, nb_=shape[1]):
                        for t in range(nb_):
                            e.dma_start(out=dst[:, :, t, :], in_=src[:, :, t, :]).then_inc(sem, 16)
                    P.add("pool", f2, r=[], w=["ws%d" % slot], key="ws%d" % slot, inc=16 * shape[1], multi=True)
                wq["issued"] += 1

        def w_next(shape):
            k = wq["i"]
            assert tuple(wq["plan"][k][1]) == tuple(shape), (k, wq["plan"][k][1], shape)
            w_issue_upto(k + 3)
            wq["i"] += 1
            slot = k % 3
            nel = int(np.prod(shape))
            v = WS[slot][:, 0:nel]
            if len(shape) == 2:
                v = v.rearrange("p (a b) -> p a b", b=shape[1])
            else:
                v = v.rearrange("p (a b c) -> p a b c", b=shape[1], c=shape[2])
            return v, "ws%d" % slot

        WIN_PIECES = [(0, 384), (384, 384), (768, 384), (1152, 416), (1568, 384), (1952, 256)]
        SW_PIECES = [(0, 160), (160, 384)]

        def plan_all():
            pl = []
            for g in range(NGROUPS_RUN):
                for l in range(NLAYERS_RUN):
                    for (c0, n) in WIN_PIECES:
                        pl.append((win_d[l].rearrange("(k p) c -> p k c", p=128)[:, :, c0:c0 + n], (8, n)))
                    if g == 1:
                        for (c0, n) in SW_PIECES:
                            pl.append((winsw_d[l].rearrange("(k p) c -> p k c", p=128)[:, :, c0:c0 + n], (8, n)))
                    if BISECT:
                        continue
                    for m in range(4):
                        pl.append((wout_d[l].rearrange("(k p) c -> p k c", p=128)[:, :, m * 256:(m + 1) * 256], (8, 256)))
                    for j in range(22):
                        pl.append((wup_d[l].rearrange("(k p) (t j c) -> p k t j c", p=128, t=2, c=128)[:, :, :, j, :], (8, 2, 128)))
                    for m in range(8):
                        pl.append((wdn_d[l].rearrange("(j p) c -> p j c", p=128)[:, :, m * 128:(m + 1) * 128], (22, 128)))
            return pl

        wq["plan"] = plan_all()

        def mset(ap, v, w):
            P.add("pool", REC.memset(ap, v), w=w)

        mset(ones[:], 1.0, ["ones"])
        mset(blk[:], 0.0, ["blk"])
        mset(blk[0:64, 0:64], 1.0, ["blk"])
        mset(blk[64:128, 64:128], 1.0, ["blk"])
        mset(zero[:], 0.0, ["zero"])
        dma("sp", pp[:], pp_d, [], ["pp"], "c_pp")
        dma("sp", tmr[:], tm_d, [], ["tmr"], "c_tm")
        dma("sp", condT[:], cond_d.rearrange("(k p) c -> p k c", p=128), [], ["condT"], "c_cond")
        dma("sp", dsh[:], dsh_d, [], ["dsh"], "c_dsh")
        dma("pool", rope[:], rope_d, [], ["rope"], "c_rope")
        dma("pool", colmask[:], colmask_d, [], ["colmask"], "c_cm")
        dma("pool", rowsel[:], rowsel_d, [], ["rowsel"], "c_rs")
        dma("pool", maskrow[:], maskrow_d, [], ["maskrow"], "c_mr")
        P.add("act", REC.activation(out=condT[:], in_=condT[:], func=AF.Silu), r=["condT"], w=["condT"])

        mod_state = dict(done=0, issued=0)
        NMOD = NLAYERS_RUN * 48

        def mods_issue(n):
            while mod_state["issued"] < min(n, NMOD):
                k = mod_state["issued"]
                l, j = divmod(k, 48)
                dma("sp", WA[k % 2][:], wada_d[l].rearrange("(k p) c -> p k c", p=128)[:, :, j * 128:(j + 1) * 128],
                    [], ["wa%d" % (k % 2)], "wa%d" % (k % 2))
                mod_state["issued"] += 1

        def mods_pump(n):
            for _ in range(n):
                k = mod_state["done"]
                if k >= NMOD:
                    return
                mods_issue(k + 2)
                l, j = divmod(k, 48)
                o = bank(7, 2)
                for kk in range(8):
                    P.add("pe", REC.matmul(o, WA[k % 2][:, kk, :], condT[:, kk, :], start=(kk == 0), stop=(kk == 7)),
                          r=["wa%d" % (k % 2), "condT"], w=[bn(7)])
                P.add("dve", REC.tensor_scalar(
                    out=mods[:, l, j, :], in0=o, scalar1=pp[:, l * NPL + 32 + j:l * NPL + 33 + j], scalar2=None, op0=ALU.add),
                    r=[bn(7), "pp"], w=["mods%d" % l])
                mod_state["done"] += 1

        def ppc(l, c, n=1):
            return pp[:, l * NPL + c:l * NPL + c + n]

        def layer_vecs(l, ci):
            for (dst, base_g, base_m, isA) in ((0, 0, 8, True), (1, 8, 16, False), (2, 16, 32, True), (3, 24, 40, False)):
                m = mods[:, l, base_m:base_m + 8, ci]
                g = ppc(l, base_g, 8)
                if isA:
                    P.add("dve", REC.scalar_tensor_tensor(
                        out=vec[:, dst, :], in0=m, scalar=1.0, in1=g, op0=ALU.add, op1=ALU.mult),
                        r=["mods%d" % l, "pp"], w=["vec"])
                else:
                    P.add("dve", REC.tensor_tensor(out=vec[:, dst, :], in0=m, in1=g, op=ALU.mult),
                          r=["mods%d" % l, "pp"], w=["vec"])

        def rstd_from_bank(bi, n, out_ap, nfeat, rn, p0=0, p1=128):
            P.add("act", REC.activation(out=out_ap, in_=bank(bi, n, p0, p1), func=AF.Sqrt, bias=EPS, scale=1.0 / nfeat),
                  r=[bn(bi)], w=[rn])
            P.add("dve", REC.reciprocal(out=out_ap, in_=out_ap), r=[rn], w=[rn])

        def norm_mod(l, ci, which):
            a_i = 0 if which == "mix" else 2
            sh_base = 0 if which == "mix" else 24
            sq = sv(0, 2048, BF16).rearrange("p (k t) -> p k t", t=512)
            for g in range(2):
                tsl = slice(g * 512, (g + 1) * 512)
                rs = sv(2048 + g * 512, 512)
                P.add("act", REC.activation(out=sq, in_=x[:, :, tsl], func=AF.Square), r=["x%d" % g], w=["sq"])
                for k in range(8):
                    P.add("pe", REC.matmul(bank(g), ones[:], sq[:, k, :], start=(k == 0), stop=(k == 7)),
                          r=["sq", "ones"], w=[bn(g)])
                rstd_from_bank(g, 512, rs, 1024.0, "rs%d" % g)
                for k in range(8):
                    t = sv(3072 + (k % 2) * 512, 512)
                    P.add("dve", REC.scalar_tensor_tensor(
                        out=t, in0=x[:, k, tsl], scalar=vec[:, a_i, k:k + 1], in1=rs, op0=ALU.mult, op1=ALU.mult),
                        r=["x%d" % g, "vec", "rs%d" % g], w=["nt%d" % (k % 2)])
                    P.add("act", REC.activation(
                        out=hy[:, k, tsl], in_=t, func=AF.Identity, bias=mods[:, l, sh_base + k, ci:ci + 1], scale=1.0),
                        r=["nt%d" % (k % 2), "mods%d" % l], w=["hy%d" % g])

        def post_norm_residual(l, which):
            c_i = 1 if which == "mix" else 3
            sq = sv(0, 2048, BF16).rearrange("p (k t) -> p k t", t=512)
            for g in range(2):
                tsl = slice(g * 512, (g + 1) * 512)
                rs = sv(2048 + g * 512, 512)
                P.add("act", REC.activation(out=sq, in_=hy[:, :, tsl], func=AF.Square), r=["hy%d" % g], w=["sq"])
                for k in range(8):
                    P.add("pe", REC.matmul(bank(g), ones[:], sq[:, k, :], start=(k == 0), stop=(k == 7)),
                          r=["sq", "ones"], w=[bn(g)])
                rstd_from_bank(g, 512, rs, 1024.0, "rs%d" % g)
                for k in range(8):
                    t = sv(3072 + (k % 2) * 512, 512)
                    P.add("dve", REC.scalar_tensor_tensor(
                        out=t, in0=hy[:, k, tsl], scalar=vec[:, c_i, k:k + 1], in1=rs, op0=ALU.mult, op1=ALU.mult),
                        r=["hy%d" % g, "vec", "rs%d" % g], w=["nt%d" % (k % 2)])
                    P.add("dve", REC.tensor_tensor(out=x[:, k, tsl], in0=x[:, k, tsl], in1=t, op=ALU.add),
                          r=["nt%d" % (k % 2), "x%d" % g], w=["x%d" % g])

        def qna(j):
            return QO[:, j, :]

        def qg(j):
            return QO[:, 3 + j, :]

        def qm(h):
            return QO[:, 6 + h, :]

        def omla(j):
            return QO[:, 10 + j, :]

        att = dict(n=0, r=0)

        def attend(q_ap, qres, keys, out_ap, ores, nq, parity, scale, obank, mask=None):
            O = bank(obank, nq)
            nk = len(keys)
            for i, (kT, kres, va, vres) in enumerate(keys):
                a = att["n"]
                att["n"] += 1
                sb_i = a % 3
                pb = sv(sb_i * 256, 256, BF16)[:, 0:nq]
                Sb = bank(sb_i, nq)

                def front(Sb=Sb, kT=kT, kres=kres, pb=pb, sb_i=sb_i):
                    P.add("pe", REC.matmul(Sb, kT, q_ap, start=True, stop=True), r=[kres, qres], w=[bn(sb_i)])
                    P.add("act", REC.activation(out=pb, in_=Sb, func=AF.Exp, scale=scale), r=[bn(sb_i)], w=["pb%d" % sb_i])

                def back(va=va, vres=vres, pb=pb, sb_i=sb_i, i=i):
                    P.add("pe", REC.matmul(O, va, pb, start=(i == 0), stop=(i == nk - 1)),
                          r=[vres, "pb%d" % sb_i], w=[bn(obank)])
                    if i == nk - 1:
                        finalize(obank, 0, nq, out_ap, ores, parity)
                pipe_submit(front, back)

        pipe = dict(q=[])
        LOOK = 2

        def pipe_submit(front, back):
            front()
            pipe["q"].append(back)
            while len(pipe["q"]) > LOOK:
                pipe["q"].pop(0)()

        def pipe_flush():
            while pipe["q"]:
                pipe["q"].pop(0)()

        def finalize(obank, c0, nq, out_ap, ores, parity):
            a = att["r"]
            att["r"] += 1
            rd = sv(768 + (a % 2) * 512, 512)
            if parity == 0:
                num = PS[obank // 2][0:64, (obank % 2) * 512 + c0:(obank % 2) * 512 + c0 + nq]
                den = PS[obank // 2][64:128, (obank % 2) * 512 + c0:(obank % 2) * 512 + c0 + nq]
                rdv = rd[0:64, 0:nq]
            else:
                num = PS[obank // 2][64:128, (obank % 2) * 512 + c0:(obank % 2) * 512 + c0 + nq]
                den = PS[obank // 2][0:64, (obank % 2) * 512 + c0:(obank % 2) * 512 + c0 + nq]
                rdv = rd[64:128, 0:nq]
            P.add("dve", REC.reciprocal(out=rdv, in_=den), r=[bn(obank)], w=["rd%d" % (a % 2)])
            P.add("dve", REC.tensor_tensor(out=out_ap, in0=num, in1=rdv, op=ALU.mult),
                  r=[bn(obank), "rd%d" % (a % 2)], w=[ores])

        def fm_mm(bi, wv, wres, c0, m, g, p1=None):
            o = bank(bi, 512, 0, m)
            for k in range(8):
                P.add("pe", REC.matmul(o, wv[:, k, c0:c0 + m], hy[:, k, g * 512:(g + 1) * 512], start=(k == 0), stop=(k == 7)),
                      r=[wres, "hy%d" % g], w=[bn(bi)])

        hn = dict(n=0)

        def head_norm(bi, g, nchunks_banks, blkmat, nfeat, gcol_l, outs, tmp_off, l):
            res = []
            sqv = sv(0, 2048, BF16).rearrange("p (k t) -> p k t", t=512)
            if len(nchunks_banks) == 2:
                p_ = 0
            else:
                p_ = hn["n"] % 2
                hn["n"] += 1
                assert nchunks_banks == [p_]
            sqn = "sqA" if p_ == 0 else "sqB"
            sb_ = 6 + p_
            rsn = "rs%d" % p_
            for n, b in enumerate(nchunks_banks):
                P.add("act", REC.activation(out=sqv[:, 2 * p_ + n, :], in_=bank(b), func=AF.Square), r=[bn(b)], w=[sqn])
            for n, b in enumerate(nchunks_banks):
                P.add("pe", REC.matmul(bank(sb_), blkmat, sqv[:, 2 * p_ + n, :], start=(n == 0), stop=(n == len(nchunks_banks) - 1)),
                      r=[sqn, "ones", "blk"], w=[bn(sb_)])
            rs = sv(2048 + p_ * 512, 512)
            rstd_from_bank(sb_, 512, rs, nfeat, rsn)
            for n, b in enumerate(nchunks_banks):
                if p_ == 0:
                    t = sv(3072 + n * 512, 512)
                    tn = "hn%d" % n
                else:
                    t = sv(1536, 512)
                    tn = "hn2"
                P.add("dve", REC.scalar_tensor_tensor(
                    out=t, in0=bank(b), scalar=ppc(l, gcol_l[n]), in1=rs, op0=ALU.mult, op1=ALU.mult),
                    r=[bn(b), "pp", rsn], w=[tn])
                res.append((t, tn))
            return res

        def cp(eng, out, in_, r, w):
            if eng == "act":
                P.add("act", REC.copy(out=out, in_=in_), r=r, w=w)
            else:
                P.add(eng, REC.tensor_copy(out=out, in_=in_), r=r, w=w)

        def tt(out, a, b, op, r, w, eng="dve"):
            P.add(eng, REC.tensor_tensor(out=out, in0=a, in1=b, op=op), r=r, w=w)

        def do_pass(grp, l):
            ci = grp
            sample = grp == 1
            layer_vecs(l, ci)
            dma("pool", wsm[:, 0, :, :], wuq_d[l].rearrange("(k p) c -> p k c", p=128), [], ["wsm"], "wsm")
            dma("pool", wsm[:, 1, :, :], wuqsw_d[l].rearrange("(k p) c -> p k c", p=128), [], ["wsm"], "wsm")
            dma("pool", wukv[:], wukv_d[l], [], ["wukv"], "wukv")
            if STOP_AFTER == "vecs":
                return
            norm_mod(l, ci, "mix")
            if STOP_AFTER == "A":
                return
            if not sample:
                knaT = kv(0, 3072).rearrange("p (j t) -> p j t", t=NT)
                Khp = kv(3072, 4096).rearrange("p (h t) -> p h t", t=NT)
                KGp = kv(7168, 2048).rearrange("p (v t) -> p v t", t=NT)
                Vp = kv(9216, 10240).rearrange("p (t s c) -> p t s c", s=10, c=128)
                VpG = kv(19456, 3072).rearrange("p (t s c) -> p t s c", s=2, c=192)
                ckvn = kv(22528, 1024)
                krT = kv(23552, 1024, 0, 32)
                mset(kv(9216, 13312), 1.0, ["Vp", "VpG"])
            else:
                stg = kv(0, 7168).rearrange("p (j t) -> p j t", t=NT)
            bi_rot = dict(n=0)

            def emit_cc(ci_):
                n_ = CH[ci_][1]
                P.add("pool", REC.collective_compute("AllGather", ALU.bypass, replica_groups=[[0, 1, 2, 3], [4, 5, 6, 7]],
                                                     ins=[sendc[ci_]], outs=[recvc[ci_][n_:5 * n_, :]]),
                      r=["send%d" % ci_, "recvpad"], w=["recv%d" % ci_], key="cc%d" % ci_, inc=1)

            def nb():
                bi_rot["n"] = (bi_rot["n"] + 1) % 4
                return bi_rot["n"]

            def tm_out(tile, wv, wres, c0, n, handler):
                o = bank(5, n)
                for k in range(8):
                    P.add("pe", REC.matmul(o, hy[:, k, tile * 128:(tile + 1) * 128], wv[:, k, c0:c0 + n], start=(k == 0), stop=(k == 7)),
                          r=[wres, "hy%d" % (tile // 4)], w=[bn(5)])
                handler(o, tile)

            ost = dict(n=0)

            def out_stage(n):
                ost["n"] += 1
                i = ost["n"] % 2
                return sv(4096 + i * 384, 384)[:, 0:n], "ost%d" % i

            def store_cache(src, sres, tile, c0, n):
                i = dma("sp", cache_d[l, tile * 128:(tile + 1) * 128, c0:c0 + n], src, [sres], [], "st_" + sres)
                stores.append(i)

            for pi, (pc0, pn) in enumerate(WIN_PIECES):
                wv, wres = w_next((8, pn))
                if pi == 0:
                    for j in range(3):
                        for g in range(2):
                            b = nb()
                            fm_mm(b, wv, wres, j * 128, 128, g)
                            cp("act", qna(j)[:, g * 512:(g + 1) * 512], bank(b), [bn(b)], ["qna%d_%d" % (j, g)])
                elif pi == 1:
                    for j in range(3):
                        for g in range(2):
                            b = nb()
                            fm_mm(b, wv, wres, j * 128, 128, g)
                            if not sample:
                                cp("act", knaT[:, j, g * 512:(g + 1) * 512], bank(b), [bn(b)], ["knaT"])
                            else:
                                cp("act", stg[:, j, g * 512:(g + 1) * 512], bank(b), [bn(b)], ["stg%d" % j])
                        if sample:
                            sa_, sn_ = send_rows(j * 128, (j + 1) * 128)
                            dma("sp", sa_, stg[:, j, :], ["stg%d" % j], [sn_], "snd%d" % j)
                    if sample:
                        emit_cc(0)
                    if not sample:
                        def h_kna(o, tile):
                            s_, sr = out_stage(384)
                            cp("act", s_, o, [bn(5)], [sr])
                            store_cache(s_, sr, tile, 0, 384)
                        for tile in range(8):
                            tm_out(tile, wv, wres, 0, 384, h_kna)
                elif pi == 2:
                    if not sample:
                        def h_vna(o, tile):
                            s_, sr = out_stage(384)
                            cp("act", s_, o, [bn(5)], [sr])
                            store_cache(s_, sr, tile, 384, 384)
                            ov = o.rearrange("p (h two d) -> p h two d", two=2, d=64)
                            cp("dve", Vp[:, tile, 0:6:2, 0:64], ov[:, :, 0, :], [bn(5)], ["Vp"])
                            cp("dve", Vp[:, tile, 1:6:2, 64:128], ov[:, :, 1, :], [bn(5)], ["Vp"])
                        for tile in range(8):
                            tm_out(tile, wv, wres, 0, 384, h_vna)
                    else:
                        def h_vna_s(o, tile):
                            s_, sr = out_stage(192)
                            sbf = s_.bitcast(BF16)
                            cp("act", sbf, o, [bn(5)], [sr])
                            sa_, sn_ = sendTM_tile(tile)
                            dma("sp", sa_[:, 0:384], sbf, [sr], [sn_], "st_" + sr)
                        for tile in range(8):
                            tm_out(tile, wv, wres, 0, 384, h_vna_s)
                elif pi == 3:
                    for g in range(2):
                        gs = slice(g * 512, (g + 1) * 512)
                        fm_mm(0, wv, wres, 0, 128, g)
                        fm_mm(1, wv, wres, 128, 128, g)
                        nr = head_norm(None, g, [0, 1], ones[:], 256.0, [80, 81], None, 0, l)
                        cqn = sv(1024, 512, BF16).rearrange("p (k t) -> p k t", t=512)
                        for n in range(2):
                            cp("act", cqn[:, n, :], nr[n][0], [nr[n][1]], ["cqn"])
                        for h in range(4):
                            b = 2 + (h % 2)
                            for kk in range(2):
                                P.add("pe", REC.matmul(bank(b, 512, 0, 96), wsm[:, 0, kk, h * 96:(h + 1) * 96], cqn[:, kk, :],
                                                                                 start=(kk == 0), stop=(kk == 1)), r=["wsm", "cqn"], w=[bn(b)])
                            if not sample:
                                cp("act", qm(h)[0:96, gs], bank(b, 512, 0, 96), [bn(b)], ["qm%d_%d" % (h, g)])
                            else:
                                for kk in range(2):
                                    P.add("pe", REC.matmul(bank(4, 512, 0, 96), wsm[:, 1, kk, h * 96:(h + 1) * 96], cqn[:, kk, :],
                                                                              start=(kk == 0), stop=(kk == 1)), r=["wsm", "cqn"], w=[bn(4)])
                                cp("act", qm(h)[0:64, gs], bank(b, 512, 0, 64), [bn(b)], ["qm%d_%d" % (h, g)])
                                t1 = sv(3072, 512, F32, 64, 96)
                                t2 = sv(3584, 512, F32, 64, 96)
                                tt(t1, bank(b, 512, 64, 96), rope[64:96, 2, gs], ALU.mult, [bn(b), "rope"], ["hn0"])
                                tt(t2, bank(4, 512, 64, 96), rope[64:96, 3, gs], ALU.mult, [bn(4), "rope"], ["hn1"])
                                tt(qm(h)[64:96, gs], t1, t2, ALU.add, ["hn0", "hn1"], ["qm%d_%d" % (h, g)])
                        b_ = hn["n"] % 2
                        fm_mm(b_, wv, wres, 256, 128, g)
                        nr = head_norm(None, g, [b_], ones[:], 128.0, [82], None, 0, l)
                        if not sample:
                            cp("act", ckvn[:, gs], nr[0][0], [nr[0][1]], ["ckvn"])
                        else:
                            cp("act", stg[:, 3, gs], nr[0][0], [nr[0][1]], ["stg3"])
                        fm_mm(2, wv, wres, 384, 32, g)
                        if not sample:
                            cp("act", krT[:, gs], bank(2, 512, 0, 32), [bn(2)], ["krT"])
                        else:
                            cp("act", stg[0:32, 4, gs], bank(2, 512, 0, 32), [bn(2)], ["stg4"])
                    if sample:
                        sa_, sn_ = send_rows(384, 512)
                        dma("sp", sa_, stg[:, 3, :], ["stg3"], [sn_], "snd3")
                        emit_cc(1)
                    if not sample:
                        for h in range(4):
                            for g in range(2):
                                gs = slice(g * 512, (g + 1) * 512)
                                b = nb()
                                P.add("pe", REC.matmul(bank(b, 512, 0, 64), wukv[:, h * 128:h * 128 + 64], ckvn[:, gs], start=True, stop=True),
                                      r=["wukv", "ckvn"], w=[bn(b)])
                                cp("act", Khp[0:64, h, gs], bank(b, 512, 0, 64), [bn(b)], ["Khp%d" % h])
                            cp("dve", Khp[64:96, h, :], krT[:, :], ["krT"], ["Khp%d" % h])
                        wv4 = wukv[:].rearrange("p (h two d) -> p h two d", two=2, d=64)[:, :, 1, :]
                        for tile in range(8):
                            P.add("pe", REC.matmul(bank(5, 256).rearrange("p (h d) -> p h d", d=64), ckvn[:, tile * 128:(tile + 1) * 128], wv4, start=True, stop=True),
                                  r=["wukv", "ckvn"], w=[bn(5)])
                            ov = bank(5, 256).rearrange("p (h two d) -> p h two d", two=2, d=64)
                            cp("dve", Vp[:, tile, 6:10:2, 0:64], ov[:, :, 0, :], [bn(5)], ["Vp"])
                            cp("dve", Vp[:, tile, 7:10:2, 64:128], ov[:, :, 1, :], [bn(5)], ["Vp"])
                        def h_ckv(o, tile):
                            s_, sr = out_stage(160)
                            ss = sv(4864, 4)
                            junk = sv(4872, 128)
                            P.add("act", REC.activation(out=junk, in_=o[:, 0:128], func=AF.Square, accum_out=ss[:, 0:1]), r=[bn(5)], w=["tmss"])
                            P.add("act", REC.activation(out=ss[:, 1:2], in_=ss[:, 0:1], func=AF.Sqrt, bias=EPS, scale=1.0 / 128), r=["tmss"], w=["tmss"])
                            P.add("dve", REC.reciprocal(out=ss[:, 1:2], in_=ss[:, 1:2]), r=["tmss"], w=["tmss"])
                            P.add("dve", REC.scalar_tensor_tensor(out=s_[:, 0:128], in0=o[:, 0:128], scalar=ss[:, 1:2], in1=tmr[:, l * 192:l * 192 + 128],
                                                                          op0=ALU.mult, op1=ALU.mult), r=[bn(5), "tmss", "tmr"], w=[sr])
                            cp("act", s_[:, 128:160], o[:, 128:160], [bn(5)], [sr])
                            store_cache(s_, sr, tile, 768, 160)
                        for tile in range(8):
                            tm_out(tile, wv, wres, 256, 160, h_ckv)
                elif pi == 4:
                    for j in range(3):
                        for g in range(2):
                            gs = slice(g * 512, (g + 1) * 512)
                            b_ = hn["n"] % 2
                            fm_mm(b_, wv, wres, j * 128, 128, g)
                            nr = head_norm(None, g, [b_], blk[:], 64.0, [83], None, 0, l)
                            cp("act", qg(j)[:, gs], nr[0][0], [nr[0][1]], ["qg%d_%d" % (j, g)])
                elif pi == 5:
                    for g in range(2):
                        gs = slice(g * 512, (g + 1) * 512)
                        b_ = hn["n"] % 2
                        fm_mm(b_, wv, wres, 0, 128, g)
                        nr = head_norm(None, g, [b_], blk[:], 64.0, [85], None, 0, l)
                        if not sample:
                            t_ = nr[0][0]
                            cp("act", KGp[0:64, 0, gs], t_[0:64, :], [nr[0][1]], ["KGp"])
                            cp("act", KGp[64:128, 0, gs], t_[0:64, :], [nr[0][1]], ["KGp"])
                            cp("act", KGp[0:64, 1, gs], t_[64:128, :], [nr[0][1]], ["KGp"])
                            cp("act", KGp[64:128, 1, gs], t_[64:128, :], [nr[0][1]], ["KGp"])
                        else:
                            cp("act", stg[:, 5, gs], nr[0][0], [nr[0][1]], ["stg5"])
                    if not sample:
                        def h_kg(o, tile):
                            s_, sr = out_stage(256)
                            ss = sv(4864, 4)
                            junk = sv(4872, 128)
                            for hh in range(2):
                                P.add("act", REC.activation(out=junk[:, 0:64], in_=o[:, hh * 64:(hh + 1) * 64], func=AF.Square, accum_out=ss[:, hh:hh + 1]),
                                      r=[bn(5)], w=["tmss"])
                            P.add("act", REC.activation(out=ss[:, 2:4], in_=ss[:, 0:2], func=AF.Sqrt, bias=EPS, scale=1.0 / 64), r=["tmss"], w=["tmss"])
                            P.add("dve", REC.reciprocal(out=ss[:, 2:4], in_=ss[:, 2:4]), r=["tmss"], w=["tmss"])
                            for hh in range(2):
                                P.add("dve", REC.scalar_tensor_tensor(out=s_[:, hh * 64:(hh + 1) * 64], in0=o[:, hh * 64:(hh + 1) * 64], scalar=ss[:, 2 + hh:3 + hh],
                                                                                    in1=tmr[:, l * 192 + 128:l * 192 + 192], op0=ALU.mult, op1=ALU.mult),
                                      r=[bn(5), "tmss", "tmr"], w=[sr])
                            cp("act", s_[:, 128:256], o[:, 128:256], [bn(5)], [sr])
                            store_cache(s_, sr, tile, 928, 256)
                            cp("dve", VpG[:, tile, :, 64:128], o[:, 128:256].rearrange("p (v d) -> p v d", d=64), [bn(5)], ["VpG"])
                        for tile in range(8):
                            tm_out(tile, wv, wres, 0, 256, h_kg)
                    else:
                        def h_vg_s(o, tile):
                            s_, sr = out_stage(64)
                            sbf = s_.bitcast(BF16)
                            cp("act", sbf, o, [bn(5)], [sr])
                            sa_, sn_ = sendTM_tile(tile)
                            dma("sp", sa_[:, 384:512], sbf, [sr], [sn_], "st_" + sr)
                        for tile in range(8):
                            tm_out(tile, wv, wres, 128, 128, h_vg_s)
                        emit_cc(3)
                        emit_cc(4)

                        def f_win(e, sem):
                            rk = get_rk(e)
                            for i_ in (0, 1, 3, 4):
                                e.dma_start(out=nawins[i_], in_=recvc[i_][bass.ds(rk * 256, 768), :]).then_inc(sem, 16)
                        P.add("pool", f_win, r=RECV_ALL, w=["nawin"], key="nawin", inc=64, multi=True)
            if sample:
                wv, wres = w_next((8, 160))
                for g in range(2):
                    gs = slice(g * 512, (g + 1) * 512)
                    fm_mm(2, wv, wres, 0, 32, g)
                    t1 = sv(3072, 512, F32, 0, 32)
                    t2 = sv(3584, 512, F32, 0, 32)
                    tt(t1, bank(2, 512, 0, 32), rope[0:32, 3, gs], ALU.mult, [bn(2), "rope"], ["hn0"])
                    tt(t2, stg[0:32, 4, gs], rope[0:32, 2, gs], ALU.mult, ["stg4", "rope"], ["hn1"])
                    tt(stg[0:32, 4, gs], t1, t2, ALU.add, ["hn0", "hn1"], ["stg4"])
                    b_ = hn["n"] % 2
                    fm_mm(b_, wv, wres, 32, 128, g)
                    nr = head_norm(None, g, [b_], blk[:], 64.0, [86], None, 0, l)
                    t1 = sv(3584, 512)
                    t2 = sv(5000, 512)
                    tt(t1, nr[0][0], rope[:, 1, gs], ALU.mult, [nr[0][1], "rope"], ["hn1"])
                    tt(t2, stg[:, 5, gs], rope[:, 0, gs], ALU.mult, ["stg5", "rope"], ["ropet2"])
                    tt(stg[:, 5, gs], t1, t2, ALU.add, ["hn1", "ropet2"], ["stg5"])
                sa_, sn_ = send_rows(512, 544)
                dma("sp", sa_, stg[0:32, 4, :], ["stg4"], [sn_], "snd4")
                sa_, sn_ = send_rows(544, 672)
                dma("sp", sa_, stg[:, 5, :], ["stg5"], [sn_], "snd5")
                emit_cc(2)
                wv, wres = w_next((8, 384))
                for j in range(3):
                    for g in range(2):
                        gs = slice(g * 512, (g + 1) * 512)
                        b_ = hn["n"] % 2
                        fm_mm(b_, wv, wres, j * 128, 128, g)
                        nr = head_norm(None, g, [b_], blk[:], 64.0, [84], None, 0, l)
                        t1 = sv(3584, 512)
                        t2 = sv(5000, 512)
                        tt(t1, nr[0][0], rope[:, 1, gs], ALU.mult, [nr[0][1], "rope"], ["hn1"])
                        tt(t2, qg(j)[:, gs], rope[:, 0, gs], ALU.mult, ["qg%d_%d" % (j, g), "rope"], ["ropet2"])
                        tt(qg(j)[:, gs], t1, t2, ALU.add, ["hn1", "ropet2"], ["qg%d_%d" % (j, g)])
            P.barrier()
            if STOP_AFTER == "B":
                return
            if not sample:
                for s in range(4):
                    qs = slice(s * 256, (s + 1) * 256)
                    g = s // 2
                    for h in range(6):
                        hs = slice((h % 2) * 64, (h % 2) * 64 + 64)
                        keys = [(knaT[hs, h // 2, s * 256 + c * 128:s * 256 + (c + 1) * 128], "knaT", Vp[:, s * 2 + c, h, :], "Vp") for c in range(2)]
                        attend(qna(h // 2)[hs, qs], "qna%d_%d" % (h // 2, g), keys, qna(h // 2)[hs, qs], "qna%d_%d" % (h // 2, g), 256, h % 2, 0.125, 3 + (att["n"] // 2) % 2)
                    for h in range(4):
                        hs = slice((h % 2) * 64, (h % 2) * 64 + 64)
                        keys = [(Khp[0:96, h, s * 256 + c * 128:s * 256 + (c + 1) * 128], "Khp%d" % h, Vp[:, s * 2 + c, 6 + h, :], "Vp") for c in range(2)]
                        attend(qm(h)[0:96, qs], "qm%d_%d" % (h, g), keys, omla(h // 2)[hs, qs], "omla%d_%d" % (h // 2, g), 256, h % 2, 96.0 ** -0.5, 3 + (att["n"] // 2) % 2)
                    for h in range(6):
                        hs = slice((h % 2) * 64, (h % 2) * 64 + 64)
                        kvh = h // 3
                        vs = slice(64, 192) if h % 2 == 0 else slice(0, 128)
                        keys = [(KGp[hs, kvh, s * 256 + c * 128:s * 256 + (c + 1) * 128], "KGp", VpG[:, s * 2 + c, kvh, vs], "VpG") for c in range(2)]
                        attend(qg(h // 2)[hs, qs], "qg%d_%d" % (h // 2, g), keys, qg(h // 2)[hs, qs], "qg%d_%d" % (h // 2, g), 256, h % 2, 0.125, 3 + (att["n"] // 2) % 2)
            else:
                sample_attention(l)
            pipe_flush()
            P.barrier()
            if STOP_AFTER == "C":
                return
            def orhs(k, gs):
                if k < 3:
                    return qna(k)[:, gs]
                if k < 5:
                    return omla(k - 3)[:, gs]
                return qg(k - 5)[:, gs]
            for m2 in range(4):
                wv, wres = w_next((8, 256))
                for mm_ in range(2):
                    m = m2 * 2 + mm_
                    for g in range(2):
                        gs = slice(g * 512, (g + 1) * 512)
                        b = (m * 2 + g) % 4
                        for k in range(8):
                            P.add("pe", REC.matmul(bank(b), wv[:, k, mm_ * 128:(mm_ + 1) * 128], orhs(k, gs), start=(k == 0), stop=(k == 7)),
                                  r=[wres, "QOall"], w=[bn(b)])
                        cp("act", hy[:, m, gs], bank(b), [bn(b)], ["hy%d" % g])
            post_norm_residual(l, "mix")
            if STOP_AFTER == "D":
                return
            norm_mod(l, ci, "ffn")
            P.barrier()
            aT = KV[:, 0:22 * NT].rearrange("p (j t) -> p j t", t=NT)
            hh_ = sv(5600, 16, BF16).rearrange("p (k c) -> p k c", c=4)
            if sample:
                def f_hs(e, sem):
                    for c_, t_ in ((0, 0), (1, NT - 1)):
                        e.dma_start(out=hsend[c_:c_ + 1, :].rearrange("c (k p) -> p k c", p=128), in_=hy[:, :, t_:t_ + 1], allow_slow_non_contiguous=True).then_inc(sem, 16)
                P.add("sp", f_hs, r=["hy0", "hy1"], w=["hsend"], key="hs", inc=32, multi=True)
                P.add("pool", REC.collective_compute("AllGather", ALU.bypass, replica_groups=[[0, 1, 2, 3], [4, 5, 6, 7]],
                                                             ins=[hsend], outs=[hrecv[2:10, :]]),
                      r=["hsend", "recvpad"], w=["hrecv"], key="cc2", inc=1)

                def f_h(e, sem, hh_=hh_):
                    rk = get_rk(e)
                    hw = hrecv[bass.ds(rk * 2, 6), :]
                    e.dma_start(out=hh_[:, :, 0:1], in_=hw[1:2, :].rearrange("c (k p) -> p k c", p=128), allow_slow_non_contiguous=True).then_inc(sem, 16)
                    e.dma_start(out=hh_[:, :, 3:4], in_=hw[4:5, :].rearrange("c (k p) -> p k c", p=128), allow_slow_non_contiguous=True).then_inc(sem, 16)
                P.add("pool", f_h, r=["hrecv"], w=["hh"], key="hh", inc=32, multi=True)
                cp("dve", hh_[:, :, 1:2], hy[:, :, 512:513], ["hy1"], ["hh"])
                cp("dve", hh_[:, :, 2:3], hy[:, :, 511:512], ["hy0"], ["hh"])
            cw = lambda t, c: ppc(l, 87 + t * 44 + c)
            cb = lambda c: ppc(l, 219 + c)
            for j in range(22):
                wv, wres = w_next((8, 2, 128))
                if sample:
                    uh = sv(5616, 8)
                    for t in range(2):
                        for k in range(8):
                            P.add("pe", REC.matmul(bank(6, 4)[:, 0:4] if t == 0 else PS[3][:, 4:8], wv[:, k, t, :], hh_[:, k, :], start=(k == 0), stop=(k == 7)),
                                  r=[wres, "hh"], w=[bn(6)])
                    cp("dve", uh, PS[3][:, 0:8], [bn(6)], ["uh"])
                for g in range(2):
                    gs = slice(g * 512, (g + 1) * 512)
                    st_ = (j * 2 + g) % 3
                    accs = []
                    for t in range(2):
                        b = st_ * 2 + t
                        c = t * 22 + j
                        for k in range(8):
                            P.add("pe", REC.matmul(bank(b), wv[:, k, t, :], hy[:, k, gs], start=(k == 0), stop=(k == 7)),
                                  r=[wres, "hy%d" % g], w=[bn(b)])
                        acc = sv(t * 512 + ((j * 2 + g) % 2) * 1024, 512)
                        an = "acc%d_%d" % (t, (j * 2 + g) % 2)
                        u = bank(b)
                        P.add("act", REC.activation(out=acc, in_=u, func=AF.Identity, bias=cb(c), scale=cw(1, c)), r=[bn(b), "pp"], w=[an])
                        if not sample:
                            a3 = acc.rearrange("p (s t) -> p s t", t=256)
                            u3 = u.rearrange("p (s t) -> p s t", t=256)
                            P.add("dve", REC.scalar_tensor_tensor(out=a3[:, :, 1:256], in0=u3[:, :, 0:255], scalar=cw(0, c), in1=a3[:, :, 1:256], op0=ALU.mult, op1=ALU.add),
                                  r=[bn(b), "pp", an], w=[an])
                            P.add("dve", REC.scalar_tensor_tensor(out=a3[:, :, 0:255], in0=u3[:, :, 1:256], scalar=cw(2, c), in1=a3[:, :, 0:255], op0=ALU.mult, op1=ALU.add),
                                  r=[bn(b), "pp", an], w=[an])
                        else:
                            P.add("dve", REC.scalar_tensor_tensor(out=acc[:, 1:512], in0=u[:, 0:511], scalar=cw(0, c), in1=acc[:, 1:512], op0=ALU.mult, op1=ALU.add),
                                  r=[bn(b), "pp", an], w=[an])
                            P.add("dve", REC.scalar_tensor_tensor(out=acc[:, 0:511], in0=u[:, 1:512], scalar=cw(2, c), in1=acc[:, 0:511], op0=ALU.mult, op1=ALU.add),
                                  r=[bn(b), "pp", an], w=[an])
                            lc = t * 4 + (0 if g == 0 else 2)
                            rc = t * 4 + (1 if g == 0 else 3)
                            P.add("dve", REC.scalar_tensor_tensor(out=acc[:, 0:1], in0=uh[:, lc:lc + 1], scalar=cw(0, c), in1=acc[:, 0:1], op0=ALU.mult, op1=ALU.add),
                                  r=["uh", "pp", an], w=[an])
                            P.add("dve", REC.scalar_tensor_tensor(out=acc[:, 511:512], in0=uh[:, rc:rc + 1], scalar=cw(2, c), in1=acc[:, 511:512], op0=ALU.mult, op1=ALU.add),
                                  r=["uh", "pp", an], w=[an])
                        accs.append((acc, an))
                    sg = sv(4096 + ((j * 2 + g) % 2) * 512, 512)
                    sn = "sg%d" % ((j * 2 + g) % 2)
                    P.add("act", REC.activation(out=sg, in_=accs[0][0], func=AF.Silu), r=[accs[0][1]], w=[sn])
                    tt(aT[:, j, gs], sg, accs[1][0], ALU.mult, [sn, accs[1][1]], ["aT%d" % g])
                if not sample:
                    mods_pump(3 if l + 1 < NLAYERS_RUN else 0)
            for m in range(8):
                wv, wres = w_next((22, 128))
                for g in range(2):
                    gs = slice(g * 512, (g + 1) * 512)
                    b = (m * 2 + g) % 6
                    for j in range(22):
                        P.add("pe", REC.matmul(bank(b), wv[:, j, :], aT[:, j, gs], start=(j == 0), stop=(j == 21)),
                              r=[wres, "aT%d" % g], w=[bn(b)])
                    cp("act", hy[:, m, gs], bank(b), [bn(b)], ["hy%d" % g])
            post_norm_residual(l, "ffn")
            P.barrier()

        def sample_attention(l):
            NK = 4352
            qall = ["qna%d_%d" % (j, g) for j in range(3) for g in range(2)]

            def group_load(tag, items, r, w):
                def f(e, sem, items=items):
                    for dst, src in items:
                        e.dma_start(out=dst, in_=src).then_inc(sem, 16)
                P.add("pool", f, r=r, w=w, key=tag, inc=16 * len(items), multi=True)

            def pool_dyn(fn, r, w, key, n):
                P.add("pool", fn, r=r, w=w, key=key, inc=16 * n, multi=True)

            KG = kv(15232, NK)
            VG = kv(19584, 34 * 192).rearrange("p (c d) -> p c d", d=192)
            mset(VG, 1.0, ["VG"])

            def gqa_loads(kvh):
                it_ = []
                for half in range(2):
                    ph = slice(half * 64, half * 64 + 64)
                    it_.append((KG[ph, 0:256], ctxk_d[l, 544 + kvh * 64:608 + kvh * 64, :]))
                    it_.append((KG[ph, 256:NK].rearrange("p (r t) -> p r t", t=1024), recvF_rows(544 + kvh * 64, 608 + kvh * 64)))
                group_load("KGc", it_, RECV_ALL, ["KG"])
                it_ = [(VG[:, 0:2, 64:128], ctxv_d[l].rearrange("(c p) f -> p c f", p=128)[:, :, 384 + kvh * 64:448 + kvh * 64])]
                for r_ in range(4):
                    for half in range(2):
                        it_.append((VG[:, 2 + r_ * 8 + half * 4:6 + r_ * 8 + half * 4, 64:128],
                                    recvT_rank_half(r_, half)[:, 384 + kvh * 64:448 + kvh * 64].rearrange("(n p) d -> p n d", p=128)))
                group_load("VGc", it_, RECV_ALL, ["VG"])
            gqa_loads(0)
            KN = kv(0, 5376).rearrange("p (j t) -> p j t", t=1792)
            VN = kv(5376, 5376).rearrange("p (c h d) -> p c h d", h=3, d=128)
            EV = kv(10752, 2880).rearrange("p (h m c) -> p h m c", m=15, c=64)
            group_load("KNc", [(KN[:, j, 0:256], ctxk_d[l, j * 128:(j + 1) * 128, :]) for j in range(3)], [], ["KN"])

            it_ = []
            for j in range(3):
                it_.append((KN[:, j, 256:512], nawinF(0, j * 128, j * 128 + 128)[:, 768:1024]))
                it_.append((KN[:, j, 512:1536], nawinF(1, j * 128, j * 128 + 128)))
                it_.append((KN[:, j, 1536:1792], nawinF(2, j * 128, j * 128 + 128)[:, 0:256]))
            group_load("KNc", it_, ["nawin"], ["KN"])
            dma("pool", rpbr[:], rpbr_d[l], [], ["rpbr"], "rpbr")
            for hg in range(2):
                pipe_flush()
                mset(VN, 1.0, ["VN"])
                it_ = []
                for hh in range(3):
                    h = hg * 3 + hh
                    po = (h % 2) * 64
                    it_.append((VN[:, 0:2, hh, po:po + 64], ctxv_d[l].rearrange("(c p) f -> p c f", p=128)[:, :, h * 64:(h + 1) * 64]))
                    for (k, half, t0, n, c0) in ((0, 1, 256, 2, 2), (1, 0, 0, 4, 4), (1, 1, 0, 4, 8), (2, 0, 0, 2, 12)):
                        src = nawinT(k, half)[t0:t0 + n * 128, h * 64:(h + 1) * 64]
                        it_.append((VN[:, c0:c0 + n, hh, po:po + 64], src.rearrange("(n p) d -> p n d", p=128)))
                group_load("VNc", it_, ["nawin"], ["VN"])
                for hh in range(3):
                    h = hg * 3 + hh
                    Eo = PS[0][0:64, :].rearrange("p (c m) -> p c m", m=16)
                    for c in range(64):
                        P.add("pe", REC.matmul(Eo[:, c, 0:15], dsh[:, 63 - c:127 - c], rpbr[:, h * 15:(h + 1) * 15], start=True, stop=True),
                              r=["dsh", "rpbr"], w=[bn(0), bn(1)])
                    et = sv(2048, 960, F32, 0, 64).rearrange("p (m c) -> p m c", c=64)
                    P.add("act", REC.activation(out=et.rearrange("p m c -> p c m"), in_=Eo[:, :, 0:15], func=AF.Exp), r=[bn(0), bn(1)], w=["et"])
                    P.add("dve", REC.tensor_tensor(out=EV[0:64, hh, :, :], in0=et, in1=colmask[0:64, :].unsqueeze(1).to_broadcast([64, 15, 64]), op=ALU.mult),
                          r=["et", "colmask"], w=["EV"])
                    cp("act", EV[64:128, hh, :, :], EV[0:64, hh, :, :], ["EV"], ["EV"])
                for hh in range(3):
                    h = hg * 3 + hh
                    par = h % 2
                    hs = slice(par * 64, par * 64 + 64)
                    qh = qna(h // 2)
                    OB = 4
                    def ctx_chunk(c, first, lastf, hs=hs, qh=qh, h=h, hh=hh):
                        for g in range(2):
                            a = att["n"]; att["n"] += 1
                            sbi = a % 3
                            Sb = bank(sbi)
                            pb = sv(sbi * 256, 256, BF16)

                            def front(Sb=Sb, pb=pb, sbi=sbi, g=g):
                                P.add("pe", REC.matmul(Sb, KN[hs, h // 2, c * 128:(c + 1) * 128], qh[hs, g * 512:(g + 1) * 512], start=True, stop=True),
                                      r=["KN"] + qall, w=[bn(sbi)])
                                P.add("act", REC.activation(out=pb, in_=Sb, func=AF.Exp, scale=0.125), r=[bn(sbi)], w=["pb%d" % sbi])

                            def back(pb=pb, sbi=sbi, g=g):
                                P.add("pe", REC.matmul(bank(OB + g), VN[:, c, hh, :], pb, start=first, stop=lastf),
                                      r=["VN", "pb%d" % sbi], w=[bn(OB + g)])
                            pipe_submit(front, back)
                    ctx_chunk(0, True, False)
                    for jj in range(12):
                        ra, rb = max(0, 2 * jj - 11), min(15, 2 * jj + 4)
                        pieces = []
                        if ra < 8:
                            pieces.append((ra, min(rb, 7)))
                        if rb >= 8:
                            pieces.append((max(ra, 8), rb))
                        for (pa, pb_) in pieces:
                            nq = (pb_ - pa + 1) * 64
                            g = pa // 8
                            a = att["n"]; att["n"] += 1
                            sbi = a % 3
                            Sb = bank(sbi, nq)
                            pb = sv(sbi * 256, 256, BF16)[:, 0:nq]
                            def front(Sb=Sb, pb=pb, sbi=sbi, jj=jj, pa=pa, pb_=pb_, nq=nq, hs=hs, qh=qh, h=h, hh=hh):
                                P.add("pe", REC.matmul(Sb, KN[hs, h // 2, 256 + jj * 128:256 + (jj + 1) * 128], qh[hs, pa * 64:pa * 64 + nq], start=True, stop=False),
                                      r=["KN"] + qall, w=[bn(sbi)])
                                P.add("pe", REC.matmul(Sb, rowsel[:, :], maskrow[:, jj * 16 + pa:jj * 16 + pb_ + 1].unsqueeze(2).to_broadcast([2, pb_ - pa + 1, 64]), start=False, stop=True),
                                      r=["rowsel", "maskrow"], w=[bn(sbi)])
                                P.add("act", REC.activation(out=pb, in_=Sb, func=AF.Exp, scale=0.125), r=[bn(sbi)], w=["pb%d" % sbi])
                                pb3 = pb.rearrange("p (r c) -> p r c", c=64)
                                for half in range(2):
                                    lk = 2 * jj - 4 + half
                                    la, lb = max(pa, lk - 7), min(pb_, lk + 7)
                                    if la > lb:
                                        continue
                                    ps_ = slice(half * 64, half * 64 + 64)
                                    P.add("dve", REC.tensor_tensor(
                                        out=pb3[ps_, la - pa:lb - pa + 1, :], in0=pb3[ps_, la - pa:lb - pa + 1, :], in1=EV[ps_, hh, la - lk + 7:lb - lk + 8, :], op=ALU.mult),
                                        r=["pb%d" % sbi, "EV"], w=["pb%d" % sbi])
                            c0 = (pa % 8) * 64

                            def back(pb=pb, sbi=sbi, jj=jj, g=g, c0=c0, nq=nq, hh=hh):
                                P.add("pe", REC.matmul(bank(OB + g)[:, c0:c0 + nq], VN[:, 2 + jj, hh, :], pb, start=False, stop=False),
                                      r=["VN", "pb%d" % sbi], w=[bn(OB + g)])
                            pipe_submit(front, back)
                    ctx_chunk(1, False, True)

                    def fin_na(qh=qh, hs=hs, h=h, par=par):
                        for g in range(2):
                            finalize(OB + g, 0, 512, qh[hs, g * 512:(g + 1) * 512], "qna%d_%d" % (h // 2, g), par)
                    pipe_submit(lambda: None, fin_na)
            for kvh in range(2):
                pipe_flush()
                if kvh == 1:
                    gqa_loads(1)
                for h in range(kvh * 3, kvh * 3 + 3):
                    par = h % 2
                    hs = slice(par * 64, par * 64 + 64)
                    vs = slice(64, 192) if par == 0 else slice(0, 128)
                    for g in range(2):
                        keys = [(KG[hs, c * 128:(c + 1) * 128], "KG", VG[:, c, vs], "VG") for c in range(34)]
                        attend(qg(h // 2)[hs, g * 512:(g + 1) * 512], "qg%d_%d" % (h // 2, g), keys, qg(h // 2)[hs, g * 512:(g + 1) * 512], "qg%d_%d" % (h // 2, g),
                               512, par, 0.125, 6 + (att["n"] // 34) % 2 if False else 6 + g)
            CK = kv(0, NK)
            Kh = kv(NK, NK)
            VM = kv(2 * NK, 34 * 192).rearrange("p (c d) -> p c d", d=192)
            group_load("CKc", [(CK[:, 0:256], ctxk_d[l, 384:512, :]),
                               (CK[:, 256:NK].rearrange("p (r t) -> p r t", t=1024), recvF_rows(384, 512))], RECV_ALL, ["CK", "KN"])
            group_load("Khc", [(Kh[64:96, 0:256], ctxk_d[l, 512:544, :]),
                               (Kh[64:96, 256:NK].rearrange("p (r t) -> p r t", t=1024), recvF_rows(512, 544))], RECV_ALL, ["Kh", "KN", "VN"])
            mset(VM, 1.0, ["VM", "VN", "EV"])
            for h in range(4):
                pipe_flush()
                par = h % 2
                hs = slice(par * 64, par * 64 + 64)
                for c9 in range(9):
                    n = 512 if c9 < 8 else 256
                    b = c9 % 3
                    P.add("pe", REC.matmul(bank(b, n, 0, 64), wukv[:, h * 128:h * 128 + 64], CK[:, c9 * 512:c9 * 512 + n], start=True, stop=True),
                          r=["wukv", "CK"], w=[bn(b)])
                    cp("act", Kh[0:64, c9 * 512:c9 * 512 + n], bank(b, n, 0, 64), [bn(b)], ["Kh", "KN", "VN"] if h == 0 else ["Kh"])
                for c8 in range(5):
                    nch = 8 if c8 < 4 else 2
                    b = 3 + c8 % 2
                    for cc in range(nch):
                        c = c8 * 8 + cc
                        P.add("pe", REC.matmul(bank(b)[:, cc * 64:(cc + 1) * 64], CK[:, c * 128:(c + 1) * 128], wukv[:, h * 128 + 64:h * 128 + 128], start=True, stop=True),
                              r=["wukv", "CK"], w=[bn(b)])
                    cp("dve", VM[:, c8 * 8:c8 * 8 + nch, 64:128], bank(b, nch * 64).rearrange("p (c d) -> p c d", d=64), [bn(b)], ["VM"])
                vs = slice(64, 192) if par == 0 else slice(0, 128)
                for g in range(2):
                    keys = [(Kh[0:96, c * 128:(c + 1) * 128], "Kh", VM[:, c, vs], "VM") for c in range(34)]
                    attend(qm(h)[0:96, g * 512:(g + 1) * 512], "qm%d_%d" % (h, g), keys, omla(h // 2)[hs, g * 512:(g + 1) * 512], "omla%d_%d" % (h // 2, g),
                           512, par, 96.0 ** -0.5, 6 + g)

        for grp in range(NGROUPS_RUN):
            for g in range(2):
                dma("sp", x[:, :, g * 512:(g + 1) * 512], xT_d.rearrange("(k p) t -> p k t", p=128)[:, :, grp * NT + g * 512:grp * NT + (g + 1) * 512],
                    [], ["x%d" % g], "xin%d" % g)
            if BISECT:
                pass
            elif grp == 0:
                mods_pump(48)
                for ci_, (o_, n_) in enumerate(CH):
                    for r0 in range(0, n_, 128):
                        n = min(128, n_ - r0)
                        dma("sp", recvc[ci_][r0:r0 + n, :], zero[0:n, :], ["zero"], ["recvpad"], "pad")
                        dma("sp", recvc[ci_][5 * n_ + r0:5 * n_ + r0 + n, :], zero[0:n, :], ["zero"], ["recvpad"], "pad")
                dma("sp", hrecv[0:2, :], zero[0:2, :], ["zero"], ["recvpad"], "pad")
                dma("sp", hrecv[10:12, :], zero[0:2, :], ["zero"], ["recvpad"], "pad")

            else:
                mods_pump(NMOD)
            for l in range(NLAYERS_RUN):
                if STOP_AFTER != "init":
                    do_pass(grp, l)
            for g in range(2):
                i = dma("sp", yT_d.rearrange("(k p) t -> p k t", p=128)[:, :, grp * NT + g * 512:grp * NT + (g + 1) * 512], x[:, :, g * 512:(g + 1) * 512],
                        ["x%d" % g], [], "yout%d" % g)
                stores.append(i)
            P.barrier()
        P.ops[-1]
        fin = P.add("sp", REC.nop(), r=[], w=[])
        P.ops[fin]["deps"].update(stores)
        P.ops[fin]["deps"].update(P.dma_since)

        P.finalize()
        keys = sorted(set(o["key"] for o in P.ops if o["key"] is not None))
        sems = {}
        for k in keys + ["E:pe", "E:act", "E:dve", "E:pool", "E:sp"]:
            sems[k] = es.enter_context(nc.semaphore("s_" + k.replace(":", "_")))
        with nc.Block() as block:
            @block.tensor
            def _(e):
                P.run("pe", e, sems)

            @block.scalar
            def _(e):
                P.run("act", e, sems)

            @block.vector
            def _(e):
                P.run("dve", e, sems)

            @block.gpsimd
            def _(e):
                P.run("pool", e, sems)

            @block.sync
            def _(e):
                P.run("sp", e, sems)
    return nc


def _perm64():
    f = np.arange(64)
    return np.where((f % 32) < 16, f + 16, f - 16)


def _perm32():
    f = np.arange(32)
    return np.where((f % 16) < 8, f + 8, f - 8)


def _rope_tables(core):
    rk = core % 4
    t = rk * NT + np.arange(NT)
    row = (t // 64).astype(np.float32)
    col = (t % 64).astype(np.float32)
    out = np.zeros((128, 4, NT), np.float32)
    fr = (10000.0 ** (-np.arange(16, dtype=np.float32) / 16)).astype(np.float32)
    for f in range(64):
        pos = row if f < 32 else col
        ang = pos * fr[f % 16]
        sgn = -1.0 if (f % 32) < 16 else 1.0
        for hh in range(2):
            out[hh * 64 + f, 0] = np.cos(ang)
            out[hh * 64 + f, 1] = sgn * np.sin(ang)
    fr8 = (10000.0 ** (-np.arange(8, dtype=np.float32) / 8)).astype(np.float32)
    for f in range(32):
        pos = row if f < 16 else col
        ang = pos * fr8[f % 8]
        sgn = -1.0 if (f % 16) < 8 else 1.0
        for base in (0, 64):
            out[base + f, 2] = np.cos(ang)
            out[base + f, 3] = sgn * np.sin(ang)
    return out


_NC_CACHE = {}


def kernel(x_prompt, x_sample, c, cache_na_k, cache_na_v, cache_mla_ckv, cache_mla_krope,
           cache_gqa_k, cache_gqa_v, c_ctx, w_ada, b_ada, g_pre_mix, g_post_mix, g_pre_ffn,
           g_post_ffn, w_in, na_rpb, mla_g_q, mla_w_uq, mla_g_kv, mla_w_ukv, gqa_g_q, gqa_g_k,
           w_out, ffn_w_up, ffn_conv_w, ffn_conv_b, ffn_w_down):
    f32 = np.float32
    A = lambda a: np.ascontiguousarray(np.asarray(a, dtype=f32))
    x_prompt, x_sample, c, c_ctx = A(x_prompt), A(x_sample), A(c), A(c_ctx)
    w_in = A(w_in)
    p64, p32 = _perm64(), _perm32()
    kr_cols = 1536 + p32
    kg_cols = 1952 + np.concatenate([hh * 64 + p64 for hh in range(2)])
    qg_cols = 1568 + np.concatenate([hh * 64 + p64 for hh in range(6)])
    w_insw = np.ascontiguousarray(w_in[:, :, np.concatenate([kr_cols, kg_cols, qg_cols])])
    w_uq = A(mla_w_uq)
    uq_cols = np.concatenate([np.concatenate([h * 96 + np.arange(64), h * 96 + 64 + p32]) for h in range(4)])
    w_uqsw = np.ascontiguousarray(w_uq[:, :, uq_cols])
    pp = np.zeros((128, L, NPL), f32)
    fm = lambda v: np.asarray(v, f32).reshape(-1, 128).T
    for l in range(L):
        pp[:, l, 0:8] = fm(g_pre_mix[l]); pp[:, l, 8:16] = fm(g_post_mix[l])
        pp[:, l, 16:24] = fm(g_pre_ffn[l]); pp[:, l, 24:32] = fm(g_post_ffn[l])
        pp[:, l, 32:80] = fm(b_ada[l])
        pp[:, l, 80:82] = fm(mla_g_q[l]); pp[:, l, 82:83] = fm(mla_g_kv[l])
        gq = np.asarray(gqa_g_q[l], f32); gk = np.asarray(gqa_g_k[l], f32)
        pp[:, l, 83] = np.tile(gq, 2); pp[:, l, 84] = np.tile(gq[p64], 2)
        pp[:, l, 85] = np.tile(gk, 2); pp[:, l, 86] = np.tile(gk[p64], 2)
        cwl = np.asarray(ffn_conv_w[l], f32)
        for t in range(3):
            pp[:, l, 87 + t * 44:87 + (t + 1) * 44] = fm(cwl[t])
        pp[:, l, 219:263] = fm(ffn_conv_b[l])
    pp = np.ascontiguousarray(pp.reshape(128, L * NPL))
    tm = np.zeros((128, L, 192), f32)
    for l in range(L):
        tm[:, l, 0:128] = np.asarray(mla_g_kv[l], f32)[None, :]
        tm[:, l, 128:192] = np.asarray(gqa_g_k[l], f32)[None, :]
    tm = np.ascontiguousarray(tm.reshape(128, L * 192))
    dsh = np.zeros((31, 127), f32)
    for j in range(31):
        dsh[j, j + 48] = 1.0
    cols = np.arange(64)
    cs = np.clip(cols - 8, 0, 48)
    cm = ((cols[:, None] >= cs[None, :]) & (cols[:, None] < cs[None, :] + 16)).astype(f32)
    colmask = np.ascontiguousarray(np.concatenate([cm, cm], 0))
    rowsel = np.zeros((2, 128), f32); rowsel[0, 0:64] = 1; rowsel[1, 64:128] = 1
    rpbr = np.ascontiguousarray(np.transpose(np.asarray(na_rpb, f32)[:, :, ::-1, :], (0, 3, 1, 2)).reshape(L, 31, 90))
    nck = np.asarray(cache_na_k, f32); ncv = np.asarray(cache_na_v, f32)
    mck = np.asarray(cache_mla_ckv, f32); mkr = np.asarray(cache_mla_krope, f32)
    gck = np.asarray(cache_gqa_k, f32); gcv = np.asarray(cache_gqa_v, f32)
    LW = NLAYERS_RUN
    if BISECT:
        w_ada = np.zeros((1, 128, 128), f32); w_out = w_ada; ffn_w_up = w_ada; ffn_w_down = w_ada
    shared = dict(w_ada=A(w_ada[:LW]), w_in=A(w_in[:LW]), w_insw=A(w_insw[:LW]), w_uq=w_uq, w_uqsw=w_uqsw, w_ukv=A(mla_w_ukv), w_out=A(w_out[:LW]),
                  w_up=A(ffn_w_up[:LW]), w_down=A(ffn_w_down[:LW]), pp=pp, tm=tm, dsh=dsh, colmask=colmask, rowsel=rowsel, rpbr=rpbr)
    in_maps = []
    for core in range(8):
        b, rk = core // 4, core % 4
        xp = x_prompt[core * 4:(core + 1) * 4].reshape(NT, D)
        xs = x_sample[b, rk * NT:(rk + 1) * NT]
        xT = np.ascontiguousarray(np.concatenate([xp, xs], 0).T)
        condT = np.ascontiguousarray(np.stack([c_ctx, c[b]], 1))
        ctxk = np.concatenate([nck[b].reshape(L, 256, 384), mck[b], mkr[b], gck[b].reshape(L, 256, 128)], -1)
        ctxk = np.ascontiguousarray(np.transpose(ctxk, (0, 2, 1)))
        ctxv = np.ascontiguousarray(np.concatenate([ncv[b].reshape(L, 256, 384), gcv[b].reshape(L, 256, 128)], -1))
        r0 = rk * 16
        mr = np.zeros((2, 12, 16), f32)
        for jj in range(12):
            for half in range(2):
                kr = r0 + 2 * jj - 4 + half
                for lr in range(16):
                    r = r0 + lr
                    rs = min(max(r - 4, 0), 56)
                    vis = (0 <= kr < 64) and (rs <= kr < rs + 8)
                    mr[half, jj, lr] = 0.0 if vis else -30000.0
        m = dict(shared)
        m.update(xT=xT, condT=condT, ctxk=ctxk, ctxv=ctxv, rope=_rope_tables(core), maskrow=np.ascontiguousarray(mr.reshape(2, 192)))
        in_maps.append(m)
    if "nc" not in _NC_CACHE:
        _NC_CACHE["nc"] = build()
    res = run_bass_kernel_spmd(_NC_CACHE["nc"], in_maps, core_ids=list(range(8)))
    y_prompt = np.zeros((32, 256, D), f32)
    y_sample = np.zeros((2, 4096, D), f32)
    caches = np.zeros((32, L, 256, R), f32)
    for core in range(8):
        r = res.results[core]
        yT = np.asarray(r["yT"], f32)
        y_prompt[core * 4:(core + 1) * 4] = yT[:, 0:NT].T.reshape(4, 256, D)
        b, rk = core // 4, core % 4
        y_sample[b, rk * NT:(rk + 1) * NT] = yT[:, NT:].T
        cc = np.asarray(r["caches"], f32).reshape(L, 4, 256, R)
        caches[core * 4:(core + 1) * 4] = np.transpose(cc, (1, 0, 2, 3))
    new_na_k = np.ascontiguousarray(caches[..., 0:384]).reshape(32, L, 256, 6, 64)
    new_na_v = np.ascontiguousarray(caches[..., 384:768]).reshape(32, L, 256, 6, 64)
    new_ckv = np.ascontiguousarray(caches[..., 768:896])
    new_kr = np.ascontiguousarray(caches[..., 896:928])
    new_gk = np.ascontiguousarray(caches[..., 928:1056]).reshape(32, L, 256, 2, 64)
    new_gv = np.ascontiguousarray(caches[..., 1056:1184]).reshape(32, L, 256, 2, 64)
    return (y_prompt, y_sample, new_na_k, new_na_v, new_ckv, new_kr, new_gk, new_gv)
```

```python
import contextlib
import numpy as np
import concourse.bass as bass
import concourse.mybir as mybir
from concourse.bass_utils import run_bass_kernel_spmd

F32 = mybir.dt.float32
BF16 = mybir.dt.bfloat16
ALU = mybir.AluOpType
AF = mybir.ActivationFunctionType

L = 4
D = 1024
NT = 1024
R = 1184
NPL = 263
EPS = 1e-6
import os
NLAYERS_RUN = int(os.environ.get('K_LAYERS', L))
NGROUPS_RUN = int(os.environ.get('K_GROUPS', 2))
STOP_AFTER = os.environ.get('K_STOP', '')
BISECT = int(os.environ.get('K_BISECT', 0))
MAXOPS = int(os.environ.get('K_MAXOPS', 10**9))


class _Rec:
    def __getattr__(self, meth):
        def mk(*a, **k):
            f = lambda e: getattr(e, meth)(*a, **k)
            f._desc = (meth, a, k)
            return f
        return mk


REC = _Rec()


class Prog:
    def __init__(self):
        self.ops = []
        self.lastw = {}
        self.rds = {}
        self.keycnt = {}
        self.pend = {}
        self.last_on = {}
        self.dma_since = []

    def add(self, eng, fn, r=(), w=(), key=None, inc=16, multi=False):
        i = len(self.ops)
        if eng in ("act", "dve"):
            extra = ["rd_" + x for x in r if x.startswith("ps") and len(x) == 3]
            if extra:
                w = list(w) + extra
        deps = set()
        for x in r:
            if x in self.lastw:
                deps.add(self.lastw[x])
        for x in w:
            if x in self.lastw:
                deps.add(self.lastw[x])
            rd = self.rds.get(x)
            if rd:
                deps.update(rd[0].values())
                deps.update(rd[1])
        if eng in self.pend:
            deps.update(self.pend.pop(eng))
        isdma = key is not None
        for x in r:
            rd = self.rds.setdefault(x, [{}, []])
            if isdma:
                rd[1].append(i)
            else:
                rd[0][eng] = i
        for x in w:
            self.lastw[x] = i
            self.rds[x] = [{}, []]
        val = None
        if isdma:
            self.keycnt[key] = self.keycnt.get(key, 0) + inc
            val = self.keycnt[key]
            self.dma_since.append(i)
        else:
            self.last_on[eng] = i
        self.ops.append(dict(eng=eng, fn=fn, deps=deps, key=key, inc=inc, val=val, sig=isdma, multi=multi))
        return i

    def barrier(self):
        b = set(self.last_on.values()) | set(self.dma_since)
        self.dma_since = []
        for e in ("pe", "act", "dve", "pool", "sp"):
            self.pend[e] = set(b) | self.pend.get(e, set())

    def finalize(self):
        ops = self.ops
        for o in ops:
            keep = set()
            for d in o["deps"]:
                if ops[d]["eng"] == "pe" and o["eng"] == "pe" and ops[d]["key"] is None:
                    continue
                ops[d]["sig"] = True
                keep.add(d)
            o["deps"] = keep
        cnt = {}
        for o in ops:
            if o["key"] is None and o["sig"]:
                cnt[o["eng"]] = cnt.get(o["eng"], 0) + 1
                o["val"] = cnt[o["eng"]]

    def run(self, eng, h, sems):
        ops = self.ops
        waited = {}
        for oi, o in enumerate(ops):
            if o["eng"] != eng:
                continue
            if oi >= MAXOPS and oi != len(ops) - 1:
                continue
            need = {}
            dl = o["deps"]
            if oi == len(ops) - 1 and MAXOPS < len(ops):
                dl = [j for j in range(min(MAXOPS, len(ops) - 1)) if ops[j]["key"] is not None]
            for d in dl:
                if d >= MAXOPS:
                    continue
                dd = ops[d]
                s = dd["key"] if dd["key"] is not None else "E:" + dd["eng"]
                if dd["val"] > need.get(s, 0):
                    need[s] = dd["val"]
            for s, v in need.items():
                if waited.get(s, 0) < v:
                    h.wait_ge(sems[s], v)
                    waited[s] = v
            if o["multi"]:
                o["fn"](h, sems[o["key"]])
                continue
            ins = o["fn"](h)
            if o["sig"]:
                if o["key"] is not None:
                    ins.then_inc(sems[o["key"]], o["inc"])
                else:
                    ins.then_inc(sems["E:" + eng], 1)


def build():
    nc = bass.Bass("TRN2", target_bir_lowering=False)
    P = Prog()

    def din(name, shape, dt=F32):
        return nc.dram_tensor(name, shape, dt, kind="ExternalInput").ap()

    xT_d = din("xT", [D, 2 * NT])
    cond_d = din("condT", [D, 2])
    ctxk_d = din("ctxk", [L, 672, 256])
    ctxv_d = din("ctxv", [L, 256, 512])
    wada_d = din("w_ada", [NLAYERS_RUN, D, 6 * D] if not BISECT else [1, 128, 128])
    win_d = din("w_in", [NLAYERS_RUN, D, 2208])
    winsw_d = din("w_insw", [NLAYERS_RUN, D, 544])
    wuq_d = din("w_uq", [L, 256, 384])
    wuqsw_d = din("w_uqsw", [L, 256, 384])
    wukv_d = din("w_ukv", [L, 128, 512])
    wout_d = din("w_out", [NLAYERS_RUN, D, D] if not BISECT else [1, 128, 128])
    wup_d = din("w_up", [NLAYERS_RUN, D, 5632] if not BISECT else [1, 128, 128])
    wdn_d = din("w_down", [NLAYERS_RUN, 2816, D] if not BISECT else [1, 128, 128])
    pp_d = din("pp", [128, L * NPL])
    tm_d = din("tm", [128, L * 192])
    rope_d = din("rope", [128, 4, NT])
    dsh_d = din("dsh", [31, 127])
    colmask_d = din("colmask", [128, 64])
    rowsel_d = din("rowsel", [2, 128])
    maskrow_d = din("maskrow", [2, 192])
    rpbr_d = din("rpbr", [L, 31, 90])
    yT_d = nc.dram_tensor("yT", [D, 2 * NT], F32, kind="ExternalOutput").ap()
    cache_d = nc.dram_tensor("caches", [L, NT, R], F32, kind="ExternalOutput").ap()
    CH = [(0, 256), (256, 256), (512, 160), (672, 256), (928, 256)]
    sendc = [nc.dram_tensor("send%d" % i, [n, NT], BF16).ap() for i, (o_, n) in enumerate(CH)]
    recvc = [nc.dram_tensor("recv%d" % i, [6 * n, NT], BF16).ap() for i, (o_, n) in enumerate(CH)]
    nawins = {i: nc.dram_tensor("nawin%d" % i, [3 * 256, NT], BF16).ap() for i in (0, 1, 3, 4)}
    RECV_ALL = ["recv%d" % i for i in range(5)]

    def chunk_of(a, b):
        for i, (o_, n) in enumerate(CH):
            if a >= o_ and b <= o_ + n:
                return i, a - o_, b - o_
        raise AssertionError((a, b))

    def send_rows(a, b):
        i, la, lb = chunk_of(a, b)
        return sendc[i][la:lb, :], "send%d" % i

    def sendTM_tile(tile):
        i, la, lb = chunk_of(672 + tile * 64, 672 + tile * 64 + 64)
        return sendc[i][la:lb, :].rearrange("r (a c) -> (r a) c", c=512), "send%d" % i

    def recvF_rows(a, b):
        i, la, lb = chunk_of(a, b)
        n = CH[i][1]
        return recvc[i][n:5 * n, :].rearrange("(r f) t -> f r t", f=n)[la:lb]

    def recvT_rank_half(r_, half):
        return recvc[3 + half][(1 + r_) * 256:(2 + r_) * 256, :].rearrange("f (a c) -> (f a) c", c=512)

    def nawinF(k, a, b):
        i, la, lb = chunk_of(a, b)
        return nawins[i][k * 256 + la:k * 256 + lb, :]

    def nawinT(k, half):
        return nawins[3 + half][k * 256:(k + 1) * 256, :].rearrange("f (a c) -> (f a) c", c=512)
    hsend_t = nc.dram_tensor("hsend", [2, D], BF16)
    hrecv_t = nc.dram_tensor("hrecv", [12, D], BF16)
    hsend = hsend_t.ap()
    hrecv = hrecv_t.ap()

    es = contextlib.ExitStack()
    with es:
        def sb(name, shape, dt):
            return es.enter_context(nc.sbuf_tensor("sb_" + name, shape, dt))

        def ps(name):
            return es.enter_context(nc.psum_tensor(name, [128, 1024], F32))

        PS = [ps("ps%d" % i) for i in range(4)]

        def bank(i, n=512, p0=0, p1=128):
            return PS[i // 2][p0:p1, (i % 2) * 512:(i % 2) * 512 + n]

        def bn(i):
            return "ps%d" % i

        x = sb("x", [128, 8, NT], F32)
        hy = sb("hy", [128, 8, NT], BF16)
        WS = [sb("ws%d" % i, [128, 3328], BF16) for i in range(3)]
        WA = [sb("wa%d" % i, [128, 8, 128], F32) for i in range(2)]
        wsm = sb("wsm", [128, 2, 2, 384], BF16)
        wukv = sb("wukv", [128, 512], BF16)
        mods = sb("mods", [128, L, 48, 2], F32)
        pp = sb("pp", [128, L * NPL], F32)
        tmr = sb("tmr", [128, L * 192], F32)
        vec = sb("vec", [128, 4, 8], F32)
        condT = sb("condT", [128, 8, 2], F32)
        ones = sb("ones", [128, 128], BF16)
        blk = sb("blk", [128, 128], BF16)
        rope = sb("rope", [128, 4, NT], BF16)
        dsh = sb("dsh", [31, 127], F32)
        colmask = sb("colmask", [128, 64], BF16)
        rowsel = sb("rowsel", [2, 128], BF16)
        maskrow = sb("maskrow", [2, 192], BF16)
        rpbr = sb("rpbr", [31, 90], F32)
        zero = sb("zero", [128, 1024], BF16)
        S = sb("S", [128, 5632], F32)
        QO = sb("QO", [128, 12, NT], BF16)
        KV = sb("KV", [128, 27648], BF16)

        def sv(off, n, dt=F32, p0=0, p1=128):
            a = S[p0:p1, off:off + n]
            return a if dt == F32 else a.bitcast(dt)

        def kv(off, n, p0=0, p1=128):
            return KV[p0:p1, off:off + n]

        def dma(eng, out, in_, r, w, key, slow=False):
            def f(e, out=out, in_=in_):
                if slow:
                    return e.dma_start(out=out, in_=in_, allow_slow_non_contiguous=True)
                return e.dma_start(out=out, in_=in_)
            return P.add(eng, f, r=r, w=w, key=key)

        stores = []
        RK = {}

        def get_rk(e):
            if "v" not in RK:
                RK["v"] = e.snap(e.partition_id() % 4)
            return RK["v"]

        wq = dict(i=0, issued=0, plan=[])

        def w_issue_upto(n):
            while wq["issued"] < min(n, len(wq["plan"])):
                k = wq["issued"]
                src, shape = wq["plan"][k]
                slot = k % 3
                nel = int(np.prod(shape))
                dst = WS[slot][:, 0:nel]
                if len(shape) == 2:
                    dst = dst.rearrange("p (a b) -> p a b", b=shape[1])
                    dma("pool", dst, src, r=[], w=["ws%d" % slot], key="ws%d" % slot)
                else:
                    dst = dst.rearrange("p (a b c) -> p a b c", b=shape[1], c=shape[2])

                    def f2(e, sem, dst=dst, src=src, nb_=shape[1]):
                        for t in range(nb_):
                            e.dma_start(out=dst[:, :, t, :], in_=src[:, :, t, :]).then_inc(sem, 16)
                    P.add("pool", f2, r=[], w=["ws%d" % slot], key="ws%d" % slot, inc=16 * shape[1], multi=True)
                wq["issued"] += 1

        def w_next(shape):
            k = wq["i"]
            assert tuple(wq["plan"][k][1]) == tuple(shape), (k, wq["plan"][k][1], shape)
            w_issue_upto(k + 3)
            wq["i"] += 1
            slot = k % 3
            nel = int(np.prod(shape))
            v = WS[slot][:, 0:nel]
            if len(shape) == 2:
                v = v.rearrange("p (a b) -> p a b", b=shape[1])
            else:
                v = v.rearrange("p (a b c) -> p a b c", b=shape[1], c=shape[2])
            return v, "ws%d" % slot

        WIN_PIECES = [(0, 384), (384, 384), (768, 384), (1152, 416), (1568, 384), (1952, 256)]
        SW_PIECES = [(0, 160), (160, 384)]

        def plan_all():
            pl = []
            for g in range(NGROUPS_RUN):
                for l in range(NLAYERS_RUN):
                    for (c0, n) in WIN_PIECES:
                        pl.append((win_d[l].rearrange("(k p) c -> p k c", p=128)[:, :, c0:c0 + n], (8, n)))
                    if g == 1:
                        for (c0, n) in SW_PIECES:
                            pl.append((winsw_d[l].rearrange("(k p) c -> p k c", p=128)[:, :, c0:c0 + n], (8, n)))
                    if BISECT:
                        continue
                    for m in range(4):
                        pl.append((wout_d[l].rearrange("(k p) c -> p k c", p=128)[:, :, m * 256:(m + 1) * 256], (8, 256)))
                    for j in range(22):
                        pl.append((wup_d[l].rearrange("(k p) (t j c) -> p k t j c", p=128, t=2, c=128)[:, :, :, j, :], (8, 2, 128)))
                    for m in range(8):
                        pl.append((wdn_d[l].rearrange("(j p) c -> p j c", p=128)[:, :, m * 128:(m + 1) * 128], (22, 128)))
            return pl

        wq["plan"] = plan_all()

        def mset(ap, v, w):
            P.add("pool", REC.memset(ap, v), w=w)

        mset(ones[:], 1.0, ["ones"])
        mset(blk[:], 0.0, ["blk"])
        mset(blk[0:64, 0:64], 1.0, ["blk"])
        mset(blk[64:128, 64:128], 1.0, ["blk"])
        mset(zero[:], 0.0, ["zero"])
        dma("sp", pp[:], pp_d, [], ["pp"], "c_pp")
        dma("sp", tmr[:], tm_d, [], ["tmr"], "c_tm")
        dma("sp", condT[:], cond_d.rearrange("(k p) c -> p k c", p=128), [], ["condT"], "c_cond")
        dma("sp", dsh[:], dsh_d, [], ["dsh"], "c_dsh")
        dma("pool", rope[:], rope_d, [], ["rope"], "c_rope")
        dma("pool", colmask[:], colmask_d, [], ["colmask"], "c_cm")
        dma("pool", rowsel[:], rowsel_d, [], ["rowsel"], "c_rs")
        dma("pool", maskrow[:], maskrow_d, [], ["maskrow"], "c_mr")
        P.add("act", REC.activation(out=condT[:], in_=condT[:], func=AF.Silu), r=["condT"], w=["condT"])

        mod_state = dict(done=0, issued=0)
        NMOD = NLAYERS_RUN * 48

        def mods_issue(n):
            while mod_state["issued"] < min(n, NMOD):
                k = mod_state["issued"]
                l, j = divmod(k, 48)
                dma("sp", WA[k % 2][:], wada_d[l].rearrange("(k p) c -> p k c", p=128)[:, :, j * 128:(j + 1) * 128],
                    [], ["wa%d" % (k % 2)], "wa%d" % (k % 2))
                mod_state["issued"] += 1

        def mods_pump(n):
            for _ in range(n):
                k = mod_state["done"]
                if k >= NMOD:
                    return
                mods_issue(k + 2)
                l, j = divmod(k, 48)
                o = bank(7, 2)
                for kk in range(8):
                    P.add("pe", REC.matmul(o, WA[k % 2][:, kk, :], condT[:, kk, :], start=(kk == 0), stop=(kk == 7)),
                          r=["wa%d" % (k % 2), "condT"], w=[bn(7)])
                P.add("dve", REC.tensor_scalar(
                    out=mods[:, l, j, :], in0=o, scalar1=pp[:, l * NPL + 32 + j:l * NPL + 33 + j], scalar2=None, op0=ALU.add),
                    r=[bn(7), "pp"], w=["mods%d" % l])
                mod_state["done"] += 1

        def ppc(l, c, n=1):
            return pp[:, l * NPL + c:l * NPL + c + n]

        def layer_vecs(l, ci):
            for (dst, base_g, base_m, isA) in ((0, 0, 8, True), (1, 8, 16, False), (2, 16, 32, True), (3, 24, 40, False)):
                m = mods[:, l, base_m:base_m + 8, ci]
                g = ppc(l, base_g, 8)
                if isA:
                    P.add("dve", REC.scalar_tensor_tensor(
                        out=vec[:, dst, :], in0=m, scalar=1.0, in1=g, op0=ALU.add, op1=ALU.mult),
                        r=["mods%d" % l, "pp"], w=["vec"])
                else:
                    P.add("dve", REC.tensor_tensor(out=vec[:, dst, :], in0=m, in1=g, op=ALU.mult),
                          r=["mods%d" % l, "pp"], w=["vec"])

        def rstd_from_bank(bi, n, out_ap, nfeat, rn, p0=0, p1=128):
            P.add("act", REC.activation(out=out_ap, in_=bank(bi, n, p0, p1), func=AF.Sqrt, bias=EPS, scale=1.0 / nfeat),
                  r=[bn(bi)], w=[rn])
            P.add("dve", REC.reciprocal(out=out_ap, in_=out_ap), r=[rn], w=[rn])

        def norm_mod(l, ci, which):
            a_i = 0 if which == "mix" else 2
            sh_base = 0 if which == "mix" else 24
            sq = sv(0, 2048, BF16).rearrange("p (k t) -> p k t", t=512)
            for g in range(2):
                tsl = slice(g * 512, (g + 1) * 512)
                rs = sv(2048 + g * 512, 512)
                P.add("act", REC.activation(out=sq, in_=x[:, :, tsl], func=AF.Square), r=["x%d" % g], w=["sq"])
                for k in range(8):
                    P.add("pe", REC.matmul(bank(g), ones[:], sq[:, k, :], start=(k == 0), stop=(k == 7)),
                          r=["sq", "ones"], w=[bn(g)])
                rstd_from_bank(g, 512, rs, 1024.0, "rs%d" % g)
                for k in range(8):
                    t = sv(3072 + (k % 2) * 512, 512)
                    P.add("dve", REC.scalar_tensor_tensor(
                        out=t, in0=x[:, k, tsl], scalar=vec[:, a_i, k:k + 1], in1=rs, op0=ALU.mult, op1=ALU.mult),
                        r=["x%d" % g, "vec", "rs%d" % g], w=["nt%d" % (k % 2)])
                    P.add("act", REC.activation(
                        out=hy[:, k, tsl], in_=t, func=AF.Identity, bias=mods[:, l, sh_base + k, ci:ci + 1], scale=1.0),
                        r=["nt%d" % (k % 2), "mods%d" % l], w=["hy%d" % g])

        def post_norm_residual(l, which):
            c_i = 1 if which == "mix" else 3
            sq = sv(0, 2048, BF16).rearrange("p (k t) -> p k t", t=512)
            for g in range(2):
                tsl = slice(g * 512, (g + 1) * 512)
                rs = sv(2048 + g * 512, 512)
                P.add("act", REC.activation(out=sq, in_=hy[:, :, tsl], func=AF.Square), r=["hy%d" % g], w=["sq"])
                for k in range(8):
                    P.add("pe", REC.matmul(bank(g), ones[:], sq[:, k, :], start=(k == 0), stop=(k == 7)),
                          r=["sq", "ones"], w=[bn(g)])
                rstd_from_bank(g, 512, rs, 1024.0, "rs%d" % g)
                for k in range(8):
                    t = sv(3072 + (k % 2) * 512, 512)
                    P.add("dve", REC.scalar_tensor_tensor(
                        out=t, in0=hy[:, k, tsl], scalar=vec[:, c_i, k:k + 1], in1=rs, op0=ALU.mult, op1=ALU.mult),
                        r=["hy%d" % g, "vec", "rs%d" % g], w=["nt%d" % (k % 2)])
                    P.add("dve", REC.tensor_tensor(out=x[:, k, tsl], in0=x[:, k, tsl], in1=t, op=ALU.add),
                          r=["nt%d" % (k % 2), "x%d" % g], w=["x%d" % g])

        def qna(j):
            return QO[:, j, :]

        def qg(j):
            return QO[:, 3 + j, :]

        def qm(h):
            return QO[:, 6 + h, :]

        def omla(j):
            return QO[:, 10 + j, :]

        att = dict(n=0, r=0)

        def attend(q_ap, qres, keys, out_ap, ores, nq, parity, scale, obank, mask=None):
            O = bank(obank, nq)
            nk = len(keys)
            for i, (kT, kres, va, vres) in enumerate(keys):
                a = att["n"]
                att["n"] += 1
                sb_i = a % 3
                pb = sv(sb_i * 256, 256, BF16)[:, 0:nq]
                Sb = bank(sb_i, nq)

                def front(Sb=Sb, kT=kT, kres=kres, pb=pb, sb_i=sb_i):
                    P.add("pe", REC.matmul(Sb, kT, q_ap, start=True, stop=True), r=[kres, qres], w=[bn(sb_i)])
                    P.add("act", REC.activation(out=pb, in_=Sb, func=AF.Exp, scale=scale), r=[bn(sb_i)], w=["pb%d" % sb_i])

                def back(va=va, vres=vres, pb=pb, sb_i=sb_i, i=i):
                    P.add("pe", REC.matmul(O, va, pb, start=(i == 0), stop=(i == nk - 1)),
                          r=[vres, "pb%d" % sb_i], w=[bn(obank)])
                    if i == nk - 1:
                        finalize(obank, 0, nq, out_ap, ores, parity)
                pipe_submit(front, back)

        pipe = dict(q=[])
        LOOK = 2

        def pipe_submit(front, back):
            front()
            pipe["q"].append(back)
            while len(pipe["q"]) > LOOK:
                pipe["q"].pop(0)()

        def pipe_flush():
            while pipe["q"]:
                pipe["q"].pop(0)()

        def finalize(obank, c0, nq, out_ap, ores, parity):
            a = att["r"]
            att["r"] += 1
            rd = sv(768 + (a % 2) * 512, 512)
            if parity == 0:
                num = PS[obank // 2][0:64, (obank % 2) * 512 + c0:(obank % 2) * 512 + c0 + nq]
                den = PS[obank // 2][64:128, (obank % 2) * 512 + c0:(obank % 2) * 512 + c0 + nq]
                rdv = rd[0:64, 0:nq]
            else:
                num = PS[obank // 2][64:128, (obank % 2) * 512 + c0:(obank % 2) * 512 + c0 + nq]
                den = PS[obank // 2][0:64, (obank % 2) * 512 + c0:(obank % 2) * 512 + c0 + nq]
                rdv = rd[64:128, 0:nq]
            P.add("dve", REC.reciprocal(out=rdv, in_=den), r=[bn(obank)], w=["rd%d" % (a % 2)])
            P.add("dve", REC.tensor_tensor(out=out_ap, in0=num, in1=rdv, op=ALU.mult),
                  r=[bn(obank), "rd%d" % (a % 2)], w=[ores])

        def fm_mm(bi, wv, wres, c0, m, g, p1=None):
            o = bank(bi, 512, 0, m)
            for k in range(8):
                P.add("pe", REC.matmul(o, wv[:, k, c0:c0 + m], hy[:, k, g * 512:(g + 1) * 512], start=(k == 0), stop=(k == 7)),
                      r=[wres, "hy%d" % g], w=[bn(bi)])

        hn = dict(n=0)

        def head_norm(bi, g, nchunks_banks, blkmat, nfeat, gcol_l, outs, tmp_off, l):
            res = []
            sqv = sv(0, 2048, BF16).rearrange("p (k t) -> p k t", t=512)
            if len(nchunks_banks) == 2:
                p_ = 0
            else:
                p_ = hn["n"] % 2
                hn["n"] += 1
                assert nchunks_banks == [p_]
            sqn = "sqA" if p_ == 0 else "sqB"
            sb_ = 6 + p_
            rsn = "rs%d" % p_
            for n, b in enumerate(nchunks_banks):
                P.add("act", REC.activation(out=sqv[:, 2 * p_ + n, :], in_=bank(b), func=AF.Square), r=[bn(b)], w=[sqn])
            for n, b in enumerate(nchunks_banks):
                P.add("pe", REC.matmul(bank(sb_), blkmat, sqv[:, 2 * p_ + n, :], start=(n == 0), stop=(n == len(nchunks_banks) - 1)),
                      r=[sqn, "ones", "blk"], w=[bn(sb_)])
            rs = sv(2048 + p_ * 512, 512)
            rstd_from_bank(sb_, 512, rs, nfeat, rsn)
            for n, b in enumerate(nchunks_banks):
                if p_ == 0:
                    t = sv(3072 + n * 512, 512)
                    tn = "hn%d" % n
                else:
                    t = sv(1536, 512)
                    tn = "hn2"
                P.add("dve", REC.scalar_tensor_tensor(
                    out=t, in0=bank(b), scalar=ppc(l, gcol_l[n]), in1=rs, op0=ALU.mult, op1=ALU.mult),
                    r=[bn(b), "pp", rsn], w=[tn])
                res.append((t, tn))
            return res

        def cp(eng, out, in_, r, w):
            if eng == "act":
                P.add("act", REC.copy(out=out, in_=in_), r=r, w=w)
            else:
                P.add(eng, REC.tensor_copy(out=out, in_=in_), r=r, w=w)

        def tt(out, a, b, op, r, w, eng="dve"):
            P.add(eng, REC.tensor_tensor(out=out, in0=a, in1=b, op=op), r=r, w=w)

        def do_pass(grp, l):
            ci = grp
            sample = grp == 1
            layer_vecs(l, ci)
            dma("pool", wsm[:, 0, :, :], wuq_d[l].rearrange("(k p) c -> p k c", p=128), [], ["wsm"], "wsm")
            dma("pool", wsm[:, 1, :, :], wuqsw_d[l].rearrange("(k p) c -> p k c", p=128), [], ["wsm"], "wsm")
            dma("pool", wukv[:], wukv_d[l], [], ["wukv"], "wukv")
            if STOP_AFTER == "vecs":
                return
            norm_mod(l, ci, "mix")
            if STOP_AFTER == "A":
                return
            if not sample:
                knaT = kv(0, 3072).rearrange("p (j t) -> p j t", t=NT)
                Khp = kv(3072, 4096).rearrange("p (h t) -> p h t", t=NT)
                KGp = kv(7168, 2048).rearrange("p (v t) -> p v t", t=NT)
                Vp = kv(9216, 10240).rearrange("p (t s c) -> p t s c", s=10, c=128)
                VpG = kv(19456, 3072).rearrange("p (t s c) -> p t s c", s=2, c=192)
                ckvn = kv(22528, 1024)
                krT = kv(23552, 1024, 0, 32)
                mset(kv(9216, 13312), 1.0, ["Vp", "VpG"])
            else:
                stg = kv(0, 7168).rearrange("p (j t) -> p j t", t=NT)
            bi_rot = dict(n=0)

            def emit_cc(ci_):
                n_ = CH[ci_][1]
                P.add("pool", REC.collective_compute("AllGather", ALU.bypass, replica_groups=[[0, 1, 2, 3], [4, 5, 6, 7]],
                                                     ins=[sendc[ci_]], outs=[recvc[ci_][n_:5 * n_, :]]),
                      r=["send%d" % ci_, "recvpad"], w=["recv%d" % ci_], key="cc%d" % ci_, inc=1)

            def nb():
                bi_rot["n"] = (bi_rot["n"] + 1) % 4
                return bi_rot["n"]

            def tm_out(tile, wv, wres, c0, n, handler):
                o = bank(5, n)
                for k in range(8):
                    P.add("pe", REC.matmul(o, hy[:, k, tile * 128:(tile + 1) * 128], wv[:, k, c0:c0 + n], start=(k == 0), stop=(k == 7)),
                          r=[wres, "hy%d" % (tile // 4)], w=[bn(5)])
                handler(o, tile)

            ost = dict(n=0)

            def out_stage(n):
                ost["n"] += 1
                i = ost["n"] % 2
                return sv(4096 + i * 384, 384)[:, 0:n], "ost%d" % i

            def store_cache(src, sres, tile, c0, n):
                i = dma("sp", cache_d[l, tile * 128:(tile + 1) * 128, c0:c0 + n], src, [sres], [], "st_" + sres)
                stores.append(i)

            for pi, (pc0, pn) in enumerate(WIN_PIECES):
                wv, wres = w_next((8, pn))
                if pi == 0:
                    for j in range(3):
                        for g in range(2):
                            b = nb()
                            fm_mm(b, wv, wres, j * 128, 128, g)
                            cp("act", qna(j)[:, g * 512:(g + 1) * 512], bank(b), [bn(b)], ["qna%d_%d" % (j, g)])
                elif pi == 1:
                    for j in range(3):
                        for g in range(2):
                            b = nb()
                            fm_mm(b, wv, wres, j * 128, 128, g)
                            if not sample:
                                cp("act", knaT[:, j, g * 512:(g + 1) * 512], bank(b), [bn(b)], ["knaT"])
                            else:
                                cp("act", stg[:, j, g * 512:(g + 1) * 512], bank(b), [bn(b)], ["stg%d" % j])
                        if sample:
                            sa_, sn_ = send_rows(j * 128, (j + 1) * 128)
                            dma("sp", sa_, stg[:, j, :], ["stg%d" % j], [sn_], "snd%d" % j)
                    if sample:
                        emit_cc(0)
                    if not sample:
                        def h_kna(o, tile):
                            s_, sr = out_stage(384)
                            cp("act", s_, o, [bn(5)], [sr])
                            store_cache(s_, sr, tile, 0, 384)
                        for tile in range(8):
                            tm_out(tile, wv, wres, 0, 384, h_kna)
                elif pi == 2:
                    if not sample:
                        def h_vna(o, tile):
                            s_, sr = out_stage(384)
                            cp("act", s_, o, [bn(5)], [sr])
                            store_cache(s_, sr, tile, 384, 384)
                            ov = o.rearrange("p (h two d) -> p h two d", two=2, d=64)
                            cp("dve", Vp[:, tile, 0:6:2, 0:64], ov[:, :, 0, :], [bn(5)], ["Vp"])
                            cp("dve", Vp[:, tile, 1:6:2, 64:128], ov[:, :, 1, :], [bn(5)], ["Vp"])
                        for tile in range(8):
                            tm_out(tile, wv, wres, 0, 384, h_vna)
                    else:
                        def h_vna_s(o, tile):
                            s_, sr = out_stage(192)
                            sbf = s_.bitcast(BF16)
                            cp("act", sbf, o, [bn(5)], [sr])
                            sa_, sn_ = sendTM_tile(tile)
                            dma("sp", sa_[:, 0:384], sbf, [sr], [sn_], "st_" + sr)
                        for tile in range(8):
                            tm_out(tile, wv, wres, 0, 384, h_vna_s)
                elif pi == 3:
                    for g in range(2):
                        gs = slice(g * 512, (g + 1) * 512)
                        fm_mm(0, wv, wres, 0, 128, g)
                        fm_mm(1, wv, wres, 128, 128, g)
                        nr = head_norm(None, g, [0, 1], ones[:], 256.0, [80, 81], None, 0, l)
                        cqn = sv(1024, 512, BF16).rearrange("p (k t) -> p k t", t=512)
                        for n in range(2):
                            cp("act", cqn[:, n, :], nr[n][0], [nr[n][1]], ["cqn"])
                        for h in range(4):
                            b = 2 + (h % 2)
                            for kk in range(2):
                                P.add("pe", REC.matmul(bank(b, 512, 0, 96), wsm[:, 0, kk, h * 96:(h + 1) * 96], cqn[:, kk, :],
                                                                                 start=(kk == 0), stop=(kk == 1)), r=["wsm", "cqn"], w=[bn(b)])
                            if not sample:
                                cp("act", qm(h)[0:96, gs], bank(b, 512, 0, 96), [bn(b)], ["qm%d_%d" % (h, g)])
                            else:
                                for kk in range(2):
                                    P.add("pe", REC.matmul(bank(4, 512, 0, 96), wsm[:, 1, kk, h * 96:(h + 1) * 96], cqn[:, kk, :],
                                                                              start=(kk == 0), stop=(kk == 1)), r=["wsm", "cqn"], w=[bn(4)])
                                cp("act", qm(h)[0:64, gs], bank(b, 512, 0, 64), [bn(b)], ["qm%d_%d" % (h, g)])
                                t1 = sv(3072, 512, F32, 64, 96)
                                t2 = sv(3584, 512, F32, 64, 96)
                                tt(t1, bank(b, 512, 64, 96), rope[64:96, 2, gs], ALU.mult, [bn(b), "rope"], ["hn0"])
                                tt(t2, bank(4, 512, 64, 96), rope[64:96, 3, gs], ALU.mult, [bn(4), "rope"], ["hn1"])
                                tt(qm(h)[64:96, gs], t1, t2, ALU.add, ["hn0", "hn1"], ["qm%d_%d" % (h, g)])
                        b_ = hn["n"] % 2
                        fm_mm(b_, wv, wres, 256, 128, g)
                        nr = head_norm(None, g, [b_], ones[:], 128.0, [82], None, 0, l)
                        if not sample:
                            cp("act", ckvn[:, gs], nr[0][0], [nr[0][1]], ["ckvn"])
                        else:
                            cp("act", stg[:, 3, gs], nr[0][0], [nr[0][1]], ["stg3"])
                        fm_mm(2, wv, wres, 384, 32, g)
                        if not sample:
                            cp("act", krT[:, gs], bank(2, 512, 0, 32), [bn(2)], ["krT"])
                        else:
                            cp("act", stg[0:32, 4, gs], bank(2, 512, 0, 32), [bn(2)], ["stg4"])
                    if sample:
                        sa_, sn_ = send_rows(384, 512)
                        dma("sp", sa_, stg[:, 3, :], ["stg3"], [sn_], "snd3")
                        emit_cc(1)
                    if not sample:
                        for h in range(4):
                            for g in range(2):
                                gs = slice(g * 512, (g + 1) * 512)
                                b = nb()
                                P.add("pe", REC.matmul(bank(b, 512, 0, 64), wukv[:, h * 128:h * 128 + 64], ckvn[:, gs], start=True, stop=True),
                                      r=["wukv", "ckvn"], w=[bn(b)])
                                cp("act", Khp[0:64, h, gs], bank(b, 512, 0, 64), [bn(b)], ["Khp%d" % h])
                            cp("dve", Khp[64:96, h, :], krT[:, :], ["krT"], ["Khp%d" % h])
                        wv4 = wukv[:].rearrange("p (h two d) -> p h two d", two=2, d=64)[:, :, 1, :]
                        for tile in range(8):
                            P.add("pe", REC.matmul(bank(5, 256).rearrange("p (h d) -> p h d", d=64), ckvn[:, tile * 128:(tile + 1) * 128], wv4, start=True, stop=True),
                                  r=["wukv", "ckvn"], w=[bn(5)])
                            ov = bank(5, 256).rearrange("p (h two d) -> p h two d", two=2, d=64)
                            cp("dve", Vp[:, tile, 6:10:2, 0:64], ov[:, :, 0, :], [bn(5)], ["Vp"])
                            cp("dve", Vp[:, tile, 7:10:2, 64:128], ov[:, :, 1, :], [bn(5)], ["Vp"])
                        def h_ckv(o, tile):
                            s_, sr = out_stage(160)
                            ss = sv(4864, 4)
                            junk = sv(4872, 128)
                            P.add("act", REC.activation(out=junk, in_=o[:, 0:128], func=AF.Square, accum_out=ss[:, 0:1]), r=[bn(5)], w=["tmss"])
                            P.add("act", REC.activation(out=ss[:, 1:2], in_=ss[:, 0:1], func=AF.Sqrt, bias=EPS, scale=1.0 / 128), r=["tmss"], w=["tmss"])
                            P.add("dve", REC.reciprocal(out=ss[:, 1:2], in_=ss[:, 1:2]), r=["tmss"], w=["tmss"])
                            P.add("dve", REC.scalar_tensor_tensor(out=s_[:, 0:128], in0=o[:, 0:128], scalar=ss[:, 1:2], in1=tmr[:, l * 192:l * 192 + 128],
                                                                          op0=ALU.mult, op1=ALU.mult), r=[bn(5), "tmss", "tmr"], w=[sr])
                            cp("act", s_[:, 128:160], o[:, 128:160], [bn(5)], [sr])
                            store_cache(s_, sr, tile, 768, 160)
                        for tile in range(8):
                            tm_out(tile, wv, wres, 256, 160, h_ckv)
                elif pi == 4:
                    for j in range(3):
                        for g in range(2):
                            gs = slice(g * 512, (g + 1) * 512)
                            b_ = hn["n"] % 2
                            fm_mm(b_, wv, wres, j * 128, 128, g)
                            nr = head_norm(None, g, [b_], blk[:], 64.0, [83], None, 0, l)
                            cp("act", qg(j)[:, gs], nr[0][0], [nr[0][1]], ["qg%d_%d" % (j, g)])
                elif pi == 5:
                    for g in range(2):
                        gs = slice(g * 512, (g + 1) * 512)
                        b_ = hn["n"] % 2
                        fm_mm(b_, wv, wres, 0, 128, g)
                        nr = head_norm(None, g, [b_], blk[:], 64.0, [85], None, 0, l)
                        if not sample:
                            t_ = nr[0][0]
                            cp("act", KGp[0:64, 0, gs], t_[0:64, :], [nr[0][1]], ["KGp"])
                            cp("act", KGp[64:128, 0, gs], t_[0:64, :], [nr[0][1]], ["KGp"])
                            cp("act", KGp[0:64, 1, gs], t_[64:128, :], [nr[0][1]], ["KGp"])
                            cp("act", KGp[64:128, 1, gs], t_[64:128, :], [nr[0][1]], ["KGp"])
                        else:
                            cp("act", stg[:, 5, gs], nr[0][0], [nr[0][1]], ["stg5"])
                    if not sample:
                        def h_kg(o, tile):
                            s_, sr = out_stage(256)
                            ss = sv(4864, 4)
                            junk = sv(4872, 128)
                            for hh in range(2):
                                P.add("act", REC.activation(out=junk[:, 0:64], in_=o[:, hh * 64:(hh + 1) * 64], func=AF.Square, accum_out=ss[:, hh:hh + 1]),
                                      r=[bn(5)], w=["tmss"])
                            P.add("act", REC.activation(out=ss[:, 2:4], in_=ss[:, 0:2], func=AF.Sqrt, bias=EPS, scale=1.0 / 64), r=["tmss"], w=["tmss"])
                            P.add("dve", REC.reciprocal(out=ss[:, 2:4], in_=ss[:, 2:4]), r=["tmss"], w=["tmss"])
                            for hh in range(2):
                                P.add("dve", REC.scalar_tensor_tensor(out=s_[:, hh * 64:(hh + 1) * 64], in0=o[:, hh * 64:(hh + 1) * 64], scalar=ss[:, 2 + hh:3 + hh],
                                                                                    in1=tmr[:, l * 192 + 128:l * 192 + 192], op0=ALU.mult, op1=ALU.mult),
                                      r=[bn(5), "tmss", "tmr"], w=[sr])
                            cp("act", s_[:, 128:256], o[:, 128:256], [bn(5)], [sr])
                            store_cache(s_, sr, tile, 928, 256)
                            cp("dve", VpG[:, tile, :, 64:128], o[:, 128:256].rearrange("p (v d) -> p v d", d=64), [bn(5)], ["VpG"])
                        for tile in range(8):
                            tm_out(tile, wv, wres, 0, 256, h_kg)
                    else:
                        def h_vg_s(o, tile):
                            s_, sr = out_stage(64)
                            sbf = s_.bitcast(BF16)
                            cp("act", sbf, o, [bn(5)], [sr])
                            sa_, sn_ = sendTM_tile(tile)
                            dma("sp", sa_[:, 384:512], sbf, [sr], [sn_], "st_" + sr)
                        for tile in range(8):
                            tm_out(tile, wv, wres, 128, 128, h_vg_s)
                        emit_cc(3)
                        emit_cc(4)

                        def f_win(e, sem):
                            rk = get_rk(e)
                            for i_ in (0, 1, 3, 4):
                                e.dma_start(out=nawins[i_], in_=recvc[i_][bass.ds(rk * 256, 768), :]).then_inc(sem, 16)
                        P.add("pool", f_win, r=RECV_ALL, w=["nawin"], key="nawin", inc=64, multi=True)
            if sample:
                wv, wres = w_next((8, 160))
                for g in range(2):
                    gs = slice(g * 512, (g + 1) * 512)
                    fm_mm(2, wv, wres, 0, 32, g)
                    t1 = sv(3072, 512, F32, 0, 32)
                    t2 = sv(3584, 512, F32, 0, 32)
                    tt(t1, bank(2, 512, 0, 32), rope[0:32, 3, gs], ALU.mult, [bn(2), "rope"], ["hn0"])
                    tt(t2, stg[0:32, 4, gs], rope[0:32, 2, gs], ALU.mult, ["stg4", "rope"], ["hn1"])
                    tt(stg[0:32, 4, gs], t1, t2, ALU.add, ["hn0", "hn1"], ["stg4"])
                    b_ = hn["n"] % 2
                    fm_mm(b_, wv, wres, 32, 128, g)
                    nr = head_norm(None, g, [b_], blk[:], 64.0, [86], None, 0, l)
                    t1 = sv(3584, 512)
                    t2 = sv(5000, 512)
                    tt(t1, nr[0][0], rope[:, 1, gs], ALU.mult, [nr[0][1], "rope"], ["hn1"])
                    tt(t2, stg[:, 5, gs], rope[:, 0, gs], ALU.mult, ["stg5", "rope"], ["ropet2"])
                    tt(stg[:, 5, gs], t1, t2, ALU.add, ["hn1", "ropet2"], ["stg5"])
                sa_, sn_ = send_rows(512, 544)
                dma("sp", sa_, stg[0:32, 4, :], ["stg4"], [sn_], "snd4")
                sa_, sn_ = send_rows(544, 672)
                dma("sp", sa_, stg[:, 5, :], ["stg5"], [sn_], "snd5")
                emit_cc(2)
                wv, wres = w_next((8, 384))
                for j in range(3):
                    for g in range(2):
                        gs = slice(g * 512, (g + 1) * 512)
                        b_ = hn["n"] % 2
                        fm_mm(b_, wv, wres, j * 128, 128, g)
                        nr = head_norm(None, g, [b_], blk[:], 64.0, [84], None, 0, l)
                        t1 = sv(3584, 512)
                        t2 = sv(5000, 512)
                        tt(t1, nr[0][0], rope[:, 1, gs], ALU.mult, [nr[0][1], "rope"], ["hn1"])
                        tt(t2, qg(j)[:, gs], rope[:, 0, gs], ALU.mult, ["qg%d_%d" % (j, g), "rope"], ["ropet2"])
                        tt(qg(j)[:, gs], t1, t2, ALU.add, ["hn1", "ropet2"], ["qg%d_%d" % (j, g)])
            P.barrier()
            if STOP_AFTER == "B":
                return
            if not sample:
                for s in range(4):
                    qs = slice(s * 256, (s + 1) * 256)
                    g = s // 2
                    for h in range(6):
                        hs = slice((h % 2) * 64, (h % 2) * 64 + 64)
                        keys = [(knaT[hs, h // 2, s * 256 + c * 128:s * 256 + (c + 1) * 128], "knaT", Vp[:, s * 2 + c, h, :], "Vp") for c in range(2)]
                        attend(qna(h // 2)[hs, qs], "qna%d_%d" % (h // 2, g), keys, qna(h // 2)[hs, qs], "qna%d_%d" % (h // 2, g), 256, h % 2, 0.125, 3 + (att["n"] // 2) % 2)
                    for h in range(4):
                        hs = slice((h % 2) * 64, (h % 2) * 64 + 64)
                        keys = [(Khp[0:96, h, s * 256 + c * 128:s * 256 + (c + 1) * 128], "Khp%d" % h, Vp[:, s * 2 + c, 6 + h, :], "Vp") for c in range(2)]
                        attend(qm(h)[0:96, qs], "qm%d_%d" % (h, g), keys, omla(h // 2)[hs, qs], "omla%d_%d" % (h // 2, g), 256, h % 2, 96.0 ** -0.5, 3 + (att["n"] // 2) % 2)
                    for h in range(6):
                        hs = slice((h % 2) * 64, (h % 2) * 64 + 64)
                        kvh = h // 3
                        vs = slice(64, 192) if h % 2 == 0 else slice(0, 128)
                        keys = [(KGp[hs, kvh, s * 256 + c * 128:s * 256 + (c + 1) * 128], "KGp", VpG[:, s * 2 + c, kvh, vs], "VpG") for c in range(2)]
                        attend(qg(h // 2)[hs, qs], "qg%d_%d" % (h // 2, g), keys, qg(h // 2)[hs, qs], "qg%d_%d" % (h // 2, g), 256, h % 2, 0.125, 3 + (att["n"] // 2) % 2)
            else:
                sample_attention(l)
            pipe_flush()
            P.barrier()
            if STOP_AFTER == "C":
                return
            def orhs(k, gs):
                if k < 3:
                    return qna(k)[:, gs]
                if k < 5:
                    return omla(k - 3)[:, gs]
                return qg(k - 5)[:, gs]
            for m2 in range(4):
                wv, wres = w_next((8, 256))
                for mm_ in range(2):
                    m = m2 * 2 + mm_
                    for g in range(2):
                        gs = slice(g * 512, (g + 1) * 512)
                        b = (m * 2 + g) % 4
                        for k in range(8):
                            P.add("pe", REC.matmul(bank(b), wv[:, k, mm_ * 128:(mm_ + 1) * 128], orhs(k, gs), start=(k == 0), stop=(k == 7)),
                                  r=[wres, "QOall"], w=[bn(b)])
                        cp("act", hy[:, m, gs], bank(b), [bn(b)], ["hy%d" % g])
            post_norm_residual(l, "mix")
            if STOP_AFTER == "D":
                return
            norm_mod(l, ci, "ffn")
            aT = KV[:, 0:22 * NT].rearrange("p (j t) -> p j t", t=NT)
            hh_ = sv(5600, 16, BF16).rearrange("p (k c) -> p k c", c=4)
            if sample:
                def f_hs(e, sem):
                    for c_, t_ in ((0, 0), (1, NT - 1)):
                        e.dma_start(out=hsend[c_:c_ + 1, :].rearrange("c (k p) -> p k c", p=128), in_=hy[:, :, t_:t_ + 1], allow_slow_non_contiguous=True).then_inc(sem, 16)
                P.add("sp", f_hs, r=["hy0", "hy1"], w=["hsend"], key="hs", inc=32, multi=True)
                P.add("pool", REC.collective_compute("AllGather", ALU.bypass, replica_groups=[[0, 1, 2, 3], [4, 5, 6, 7]],
                                                             ins=[hsend], outs=[hrecv[2:10, :]]),
                      r=["hsend", "recvpad"], w=["hrecv"], key="cc2", inc=1)

                def f_h(e, sem, hh_=hh_):
                    rk = get_rk(e)
                    hw = hrecv[bass.ds(rk * 2, 6), :]
                    e.dma_start(out=hh_[:, :, 0:1], in_=hw[1:2, :].rearrange("c (k p) -> p k c", p=128), allow_slow_non_contiguous=True).then_inc(sem, 16)
                    e.dma_start(out=hh_[:, :, 3:4], in_=hw[4:5, :].rearrange("c (k p) -> p k c", p=128), allow_slow_non_contiguous=True).then_inc(sem, 16)
                P.add("pool", f_h, r=["hrecv"], w=["hh"], key="hh", inc=32, multi=True)
                cp("dve", hh_[:, :, 1:2], hy[:, :, 512:513], ["hy1"], ["hh"])
                cp("dve", hh_[:, :, 2:3], hy[:, :, 511:512], ["hy0"], ["hh"])
            cw = lambda t, c: ppc(l, 87 + t * 44 + c)
            cb = lambda c: ppc(l, 219 + c)
            for j in range(22):
                wv, wres = w_next((8, 2, 128))
                if sample:
                    uh = sv(5616, 8)
                    for t in range(2):
                        for k in range(8):
                            P.add("pe", REC.matmul(bank(6, 4)[:, 0:4] if t == 0 else PS[3][:, 4:8], wv[:, k, t, :], hh_[:, k, :], start=(k == 0), stop=(k == 7)),
                                  r=[wres, "hh"], w=[bn(6)])
                    cp("dve", uh, PS[3][:, 0:8], [bn(6)], ["uh"])
                for g in range(2):
                    gs = slice(g * 512, (g + 1) * 512)
                    st_ = (j * 2 + g) % 3
                    accs = []
                    for t in range(2):
                        b = st_ * 2 + t
                        c = t * 22 + j
                        for k in range(8):
                            P.add("pe", REC.matmul(bank(b), wv[:, k, t, :], hy[:, k, gs], start=(k == 0), stop=(k == 7)),
                                  r=[wres, "hy%d" % g], w=[bn(b)])
                        acc = sv(t * 512 + ((j * 2 + g) % 2) * 1024, 512)
                        an = "acc%d_%d" % (t, (j * 2 + g) % 2)
                        u = bank(b)
                        P.add("act", REC.activation(out=acc, in_=u, func=AF.Identity, bias=cb(c), scale=cw(1, c)), r=[bn(b), "pp"], w=[an])
                        if not sample:
                            a3 = acc.rearrange("p (s t) -> p s t", t=256)
                            u3 = u.rearrange("p (s t) -> p s t", t=256)
                            P.add("dve", REC.scalar_tensor_tensor(out=a3[:, :, 1:256], in0=u3[:, :, 0:255], scalar=cw(0, c), in1=a3[:, :, 1:256], op0=ALU.mult, op1=ALU.add),
                                  r=[bn(b), "pp", an], w=[an])
                            P.add("dve", REC.scalar_tensor_tensor(out=a3[:, :, 0:255], in0=u3[:, :, 1:256], scalar=cw(2, c), in1=a3[:, :, 0:255], op0=ALU.mult, op1=ALU.add),
                                  r=[bn(b), "pp", an], w=[an])
                        else:
                            P.add("dve", REC.scalar_tensor_tensor(out=acc[:, 1:512], in0=u[:, 0:511], scalar=cw(0, c), in1=acc[:, 1:512], op0=ALU.mult, op1=ALU.add),
                                  r=[bn(b), "pp", an], w=[an])
                            P.add("dve", REC.scalar_tensor_tensor(out=acc[:, 0:511], in0=u[:, 1:512], scalar=cw(2, c), in1=acc[:, 0:511], op0=ALU.mult, op1=ALU.add),
                                  r=[bn(b), "pp", an], w=[an])
                            lc = t * 4 + (0 if g == 0 else 2)
                            rc = t * 4 + (1 if g == 0 else 3)
                            P.add("dve", REC.scalar_tensor_tensor(out=acc[:, 0:1], in0=uh[:, lc:lc + 1], scalar=cw(0, c), in1=acc[:, 0:1], op0=ALU.mult, op1=ALU.add),
                                  r=["uh", "pp", an], w=[an])
                            P.add("dve", REC.scalar_tensor_tensor(out=acc[:, 511:512], in0=uh[:, rc:rc + 1], scalar=cw(2, c), in1=acc[:, 511:512], op0=ALU.mult, op1=ALU.add),
                                  r=["uh", "pp", an], w=[an])
                        accs.append((acc, an))
                    sg = sv(4096 + ((j * 2 + g) % 2) * 512, 512)
                    sn = "sg%d" % ((j * 2 + g) % 2)
                    P.add("act", REC.activation(out=sg, in_=accs[0][0], func=AF.Silu), r=[accs[0][1]], w=[sn])
                    tt(aT[:, j, gs], sg, accs[1][0], ALU.mult, [sn, accs[1][1]], ["aT%d" % g])
                if not sample:
                    mods_pump(3 if l + 1 < NLAYERS_RUN else 0)
            for m in range(8):
                wv, wres = w_next((22, 128))
                for g in range(2):
                    gs = slice(g * 512, (g + 1) * 512)
                    b = (m * 2 + g) % 6
                    for j in range(22):
                        P.add("pe", REC.matmul(bank(b), wv[:, j, :], aT[:, j, gs], start=(j == 0), stop=(j == 21)),
                              r=[wres, "aT%d" % g], w=[bn(b)])
                    cp("act", hy[:, m, gs], bank(b), [bn(b)], ["hy%d" % g])
            post_norm_residual(l, "ffn")
            P.barrier()

        def sample_attention(l):
            NK = 4352
            qall = ["qna%d_%d" % (j, g) for j in range(3) for g in range(2)]

            def group_load(tag, items, r, w):
                def f(e, sem, items=items):
                    for dst, src in items:
                        e.dma_start(out=dst, in_=src).then_inc(sem, 16)
                P.add("pool", f, r=r, w=w, key=tag, inc=16 * len(items), multi=True)

            def pool_dyn(fn, r, w, key, n):
                P.add("pool", fn, r=r, w=w, key=key, inc=16 * n, multi=True)

            KG = kv(15232, NK)
            VG = kv(19584, 34 * 192).rearrange("p (c d) -> p c d", d=192)
            mset(VG, 1.0, ["VG"])

            def gqa_loads(kvh):
                it_ = []
                for half in range(2):
                    ph = slice(half * 64, half * 64 + 64)
                    it_.append((KG[ph, 0:256], ctxk_d[l, 544 + kvh * 64:608 + kvh * 64, :]))
                    it_.append((KG[ph, 256:NK].rearrange("p (r t) -> p r t", t=1024), recvF_rows(544 + kvh * 64, 608 + kvh * 64)))
                group_load("KGc", it_, RECV_ALL, ["KG"])
                it_ = [(VG[:, 0:2, 64:128], ctxv_d[l].rearrange("(c p) f -> p c f", p=128)[:, :, 384 + kvh * 64:448 + kvh * 64])]
                for r_ in range(4):
                    for half in range(2):
                        it_.append((VG[:, 2 + r_ * 8 + half * 4:6 + r_ * 8 + half * 4, 64:128],
                                    recvT_rank_half(r_, half)[:, 384 + kvh * 64:448 + kvh * 64].rearrange("(n p) d -> p n d", p=128)))
                group_load("VGc", it_, RECV_ALL, ["VG"])
            gqa_loads(0)
            KN = kv(0, 5376).rearrange("p (j t) -> p j t", t=1792)
            VN = kv(5376, 5376).rearrange("p (c h d) -> p c h d", h=3, d=128)
            EV = kv(10752, 2880).rearrange("p (h m c) -> p h m c", m=15, c=64)
            group_load("KNc", [(KN[:, j, 0:256], ctxk_d[l, j * 128:(j + 1) * 128, :]) for j in range(3)], [], ["KN"])

            it_ = []
            for j in range(3):
                it_.append((KN[:, j, 256:512], nawinF(0, j * 128, j * 128 + 128)[:, 768:1024]))
                it_.append((KN[:, j, 512:1536], nawinF(1, j * 128, j * 128 + 128)))
                it_.append((KN[:, j, 1536:1792], nawinF(2, j * 128, j * 128 + 128)[:, 0:256]))
            group_load("KNc", it_, ["nawin"], ["KN"])
            dma("pool", rpbr[:], rpbr_d[l], [], ["rpbr"], "rpbr")
            for hg in range(2):
                pipe_flush()
                mset(VN, 1.0, ["VN"])
                it_ = []
                for hh in range(3):
                    h = hg * 3 + hh
                    po = (h % 2) * 64
                    it_.append((VN[:, 0:2, hh, po:po + 64], ctxv_d[l].rearrange("(c p) f -> p c f", p=128)[:, :, h * 64:(h + 1) * 64]))
                    for (k, half, t0, n, c0) in ((0, 1, 256, 2, 2), (1, 0, 0, 4, 4), (1, 1, 0, 4, 8), (2, 0, 0, 2, 12)):
                        src = nawinT(k, half)[t0:t0 + n * 128, h * 64:(h + 1) * 64]
                        it_.append((VN[:, c0:c0 + n, hh, po:po + 64], src.rearrange("(n p) d -> p n d", p=128)))
                group_load("VNc", it_, ["nawin"], ["VN"])
                for hh in range(3):
                    h = hg * 3 + hh
                    Eo = PS[0][0:64, :].rearrange("p (c m) -> p c m", m=16)
                    for c in range(64):
                        P.add("pe", REC.matmul(Eo[:, c, 0:15], dsh[:, 63 - c:127 - c], rpbr[:, h * 15:(h + 1) * 15], start=True, stop=True),
                              r=["dsh", "rpbr"], w=[bn(0), bn(1)])
                    et = sv(2048, 960, F32, 0, 64).rearrange("p (m c) -> p m c", c=64)
                    P.add("act", REC.activation(out=et.rearrange("p m c -> p c m"), in_=Eo[:, :, 0:15], func=AF.Exp), r=[bn(0), bn(1)], w=["et"])
                    P.add("dve", REC.tensor_tensor(out=EV[0:64, hh, :, :], in0=et, in1=colmask[0:64, :].unsqueeze(1).to_broadcast([64, 15, 64]), op=ALU.mult),
                          r=["et", "colmask"], w=["EV"])
                    cp("act", EV[64:128, hh, :, :], EV[0:64, hh, :, :], ["EV"], ["EV"])
                for hh in range(3):
                    h = hg * 3 + hh
                    par = h % 2
                    hs = slice(par * 64, par * 64 + 64)
                    qh = qna(h // 2)
                    OB = 4
                    def ctx_chunk(c, first, lastf, hs=hs, qh=qh, h=h, hh=hh):
                        for g in range(2):
                            a = att["n"]; att["n"] += 1
                            sbi = a % 3
                            Sb = bank(sbi)
                            pb = sv(sbi * 256, 256, BF16)

                            def front(Sb=Sb, pb=pb, sbi=sbi, g=g):
                                P.add("pe", REC.matmul(Sb, KN[hs, h // 2, c * 128:(c + 1) * 128], qh[hs, g * 512:(g + 1) * 512], start=True, stop=True),
                                      r=["KN"] + qall, w=[bn(sbi)])
                                P.add("act", REC.activation(out=pb, in_=Sb, func=AF.Exp, scale=0.125), r=[bn(sbi)], w=["pb%d" % sbi])

                            def back(pb=pb, sbi=sbi, g=g):
                                P.add("pe", REC.matmul(bank(OB + g), VN[:, c, hh, :], pb, start=first, stop=lastf),
                                      r=["VN", "pb%d" % sbi], w=[bn(OB + g)])
                            pipe_submit(front, back)
                    ctx_chunk(0, True, False)
                    for jj in range(12):
                        ra, rb = max(0, 2 * jj - 11), min(15, 2 * jj + 4)
                        pieces = []
                        if ra < 8:
                            pieces.append((ra, min(rb, 7)))
                        if rb >= 8:
                            pieces.append((max(ra, 8), rb))
                        for (pa, pb_) in pieces:
                            nq = (pb_ - pa + 1) * 64
                            g = pa // 8
                            a = att["n"]; att["n"] += 1
                            sbi = a % 3
                            Sb = bank(sbi, nq)
                            pb = sv(sbi * 256, 256, BF16)[:, 0:nq]
                            def front(Sb=Sb, pb=pb, sbi=sbi, jj=jj, pa=pa, pb_=pb_, nq=nq, hs=hs, qh=qh, h=h, hh=hh):
                                P.add("pe", REC.matmul(Sb, KN[hs, h // 2, 256 + jj * 128:256 + (jj + 1) * 128], qh[hs, pa * 64:pa * 64 + nq], start=True, stop=False),
                                      r=["KN"] + qall, w=[bn(sbi)])
                                P.add("pe", REC.matmul(Sb, rowsel[:, :], maskrow[:, jj * 16 + pa:jj * 16 + pb_ + 1].unsqueeze(2).to_broadcast([2, pb_ - pa + 1, 64]), start=False, stop=True),
                                      r=["rowsel", "maskrow"], w=[bn(sbi)])
                                P.add("act", REC.activation(out=pb, in_=Sb, func=AF.Exp, scale=0.125), r=[bn(sbi)], w=["pb%d" % sbi])
                                pb3 = pb.rearrange("p (r c) -> p r c", c=64)
                                for half in range(2):
                                    lk = 2 * jj - 4 + half
                                    la, lb = max(pa, lk - 7), min(pb_, lk + 7)
                                    if la > lb:
                                        continue
                                    ps_ = slice(half * 64, half * 64 + 64)
                                    P.add("dve", REC.tensor_tensor(
                                        out=pb3[ps_, la - pa:lb - pa + 1, :], in0=pb3[ps_, la - pa:lb - pa + 1, :], in1=EV[ps_, hh, la - lk + 7:lb - lk + 8, :], op=ALU.mult),
                                        r=["pb%d" % sbi, "EV"], w=["pb%d" % sbi])
                            c0 = (pa % 8) * 64

                            def back(pb=pb, sbi=sbi, jj=jj, g=g, c0=c0, nq=nq, hh=hh):
                                P.add("pe", REC.matmul(bank(OB + g)[:, c0:c0 + nq], VN[:, 2 + jj, hh, :], pb, start=False, stop=False),
                                      r=["VN", "pb%d" % sbi], w=[bn(OB + g)])
                            pipe_submit(front, back)
                    ctx_chunk(1, False, True)

                    def fin_na(qh=qh, hs=hs, h=h, par=par):
                        for g in range(2):
                            finalize(OB + g, 0, 512, qh[hs, g * 512:(g + 1) * 512], "qna%d_%d" % (h // 2, g), par)
                    pipe_submit(lambda: None, fin_na)
            for kvh in range(2):
                pipe_flush()
                if kvh == 1:
                    gqa_loads(1)
                for h in range(kvh * 3, kvh * 3 + 3):
                    par = h % 2
                    hs = slice(par * 64, par * 64 + 64)
                    vs = slice(64, 192) if par == 0 else slice(0, 128)
                    for g in range(2):
                        keys = [(KG[hs, c * 128:(c + 1) * 128], "KG", VG[:, c, vs], "VG") for c in range(34)]
                        attend(qg(h // 2)[hs, g * 512:(g + 1) * 512], "qg%d_%d" % (h // 2, g), keys, qg(h // 2)[hs, g * 512:(g + 1) * 512], "qg%d_%d" % (h // 2, g),
                               512, par, 0.125, 6 + (att["n"] // 34) % 2 if False else 6 + g)
            CK = kv(0, NK)
            Kh = kv(NK, NK)
            VM = kv(2 * NK, 34 * 192).rearrange("p (c d) -> p c d", d=192)
            group_load("CKc", [(CK[:, 0:256], ctxk_d[l, 384:512, :]),
                               (CK[:, 256:NK].rearrange("p (r t) -> p r t", t=1024), recvF_rows(384, 512))], RECV_ALL, ["CK", "KN"])
            group_load("Khc", [(Kh[64:96, 0:256], ctxk_d[l, 512:544, :]),
                               (Kh[64:96, 256:NK].rearrange("p (r t) -> p r t", t=1024), recvF_rows(512, 544))], RECV_ALL, ["Kh", "KN", "VN"])
            mset(VM, 1.0, ["VM", "VN", "EV"])
            for h in range(4):
                pipe_flush()
                par = h % 2
                hs = slice(par * 64, par * 64 + 64)
                for c9 in range(9):
                    n = 512 if c9 < 8 else 256
                    b = c9 % 3
                    P.add("pe", REC.matmul(bank(b, n, 0, 64), wukv[:, h * 128:h * 128 + 64], CK[:, c9 * 512:c9 * 512 + n], start=True, stop=True),
                          r=["wukv", "CK"], w=[bn(b)])
                    cp("act", Kh[0:64, c9 * 512:c9 * 512 + n], bank(b, n, 0, 64), [bn(b)], ["Kh", "KN", "VN"] if h == 0 else ["Kh"])
                for c8 in range(5):
                    nch = 8 if c8 < 4 else 2
                    b = 3 + c8 % 2
                    for cc in range(nch):
                        c = c8 * 8 + cc
                        P.add("pe", REC.matmul(bank(b)[:, cc * 64:(cc + 1) * 64], CK[:, c * 128:(c + 1) * 128], wukv[:, h * 128 + 64:h * 128 + 128], start=True, stop=True),
                              r=["wukv", "CK"], w=[bn(b)])
                    cp("dve", VM[:, c8 * 8:c8 * 8 + nch, 64:128], bank(b, nch * 64).rearrange("p (c d) -> p c d", d=64), [bn(b)], ["VM"])
                vs = slice(64, 192) if par == 0 else slice(0, 128)
                for g in range(2):
                    keys = [(Kh[0:96, c * 128:(c + 1) * 128], "Kh", VM[:, c, vs], "VM") for c in range(34)]
                    attend(qm(h)[0:96, g * 512:(g + 1) * 512], "qm%d_%d" % (h, g), keys, omla(h // 2)[hs, g * 512:(g + 1) * 512], "omla%d_%d" % (h // 2, g),
                           512, par, 96.0 ** -0.5, 6 + g)

        for grp in range(NGROUPS_RUN):
            for g in range(2):
                dma("sp", x[:, :, g * 512:(g + 1) * 512], xT_d.rearrange("(k p) t -> p k t", p=128)[:, :, grp * NT + g * 512:grp * NT + (g + 1) * 512],
                    [], ["x%d" % g], "xin%d" % g)
            if BISECT:
                pass
            elif grp == 0:
                mods_pump(48)
                for ci_, (o_, n_) in enumerate(CH):
                    for r0 in range(0, n_, 128):
                        n = min(128, n_ - r0)
                        dma("sp", recvc[ci_][r0:r0 + n, :], zero[0:n, :], ["zero"], ["recvpad"], "pad")
                        dma("sp", recvc[ci_][5 * n_ + r0:5 * n_ + r0 + n, :], zero[0:n, :], ["zero"], ["recvpad"], "pad")
                dma("sp", hrecv[0:2, :], zero[0:2, :], ["zero"], ["recvpad"], "pad")
                dma("sp", hrecv[10:12, :], zero[0:2, :], ["zero"], ["recvpad"], "pad")

            else:
                mods_pump(NMOD)
            for l in range(NLAYERS_RUN):
                if STOP_AFTER != "init":
                    do_pass(grp, l)
            for g in range(2):
                i = dma("sp", yT_d.rearrange("(k p) t -> p k t", p=128)[:, :, grp * NT + g * 512:grp * NT + (g + 1) * 512], x[:, :, g * 512:(g + 1) * 512],
                        ["x%d" % g], [], "yout%d" % g)
                stores.append(i)
            P.barrier()
        P.ops[-1]
        fin = P.add("sp", REC.nop(), r=[], w=[])
        P.ops[fin]["deps"].update(stores)
        P.ops[fin]["deps"].update(P.dma_since)

        P.finalize()
        keys = sorted(set(o["key"] for o in P.ops if o["key"] is not None))
        sems = {}
        for k in keys + ["E:pe", "E:act", "E:dve", "E:pool", "E:sp"]:
            sems[k] = es.enter_context(nc.semaphore("s_" + k.replace(":", "_")))
        with nc.Block() as block:
            @block.tensor
            def _(e):
                P.run("pe", e, sems)

            @block.scalar
            def _(e):
                P.run("act", e, sems)

            @block.vector
            def _(e):
                P.run("dve", e, sems)

            @block.gpsimd
            def _(e):
                P.run("pool", e, sems)

            @block.sync
            def _(e):
                P.run("sp", e, sems)
    return nc


def _perm64():
    f = np.arange(64)
    return np.where((f % 32) < 16, f + 16, f - 16)


def _perm32():
    f = np.arange(32)
    return np.where((f % 16) < 8, f + 8, f - 8)


def _rope_tables(core):
    rk = core % 4
    t = rk * NT + np.arange(NT)
    row = (t // 64).astype(np.float32)
    col = (t % 64).astype(np.float32)
    out = np.zeros((128, 4, NT), np.float32)
    fr = (10000.0 ** (-np.arange(16, dtype=np.float32) / 16)).astype(np.float32)
    for f in range(64):
        pos = row if f < 32 else col
        ang = pos * fr[f % 16]
        sgn = -1.0 if (f % 32) < 16 else 1.0
        for hh in range(2):
            out[hh * 64 + f, 0] = np.cos(ang)
            out[hh * 64 + f, 1] = sgn * np.sin(ang)
    fr8 = (10000.0 ** (-np.arange(8, dtype=np.float32) / 8)).astype(np.float32)
    for f in range(32):
        pos = row if f < 16 else col
        ang = pos * fr8[f % 8]
        sgn = -1.0 if (f % 16) < 8 else 1.0
        for base in (0, 64):
            out[base + f, 2] = np.cos(ang)
            out[base + f, 3] = sgn * np.sin(ang)
    return out


_NC_CACHE = {}


def kernel(x_prompt, x_sample, c, cache_na_k, cache_na_v, cache_mla_ckv, cache_mla_krope,
           cache_gqa_k, cache_gqa_v, c_ctx, w_ada, b_ada, g_pre_mix, g_post_mix, g_pre_ffn,
           g_post_ffn, w_in, na_rpb, mla_g_q, mla_w_uq, mla_g_kv, mla_w_ukv, gqa_g_q, gqa_g_k,
           w_out, ffn_w_up, ffn_conv_w, ffn_conv_b, ffn_w_down):
    f32 = np.float32
    A = lambda a: np.ascontiguousarray(np.asarray(a, dtype=f32))
    x_prompt, x_sample, c, c_ctx = A(x_prompt), A(x_sample), A(c), A(c_ctx)
    w_in = A(w_in)
    p64, p32 = _perm64(), _perm32()
    kr_cols = 1536 + p32
    kg_cols = 1952 + np.concatenate([hh * 64 + p64 for hh in range(2)])
    qg_cols = 1568 + np.concatenate([hh * 64 + p64 for hh in range(6)])
    w_insw = np.ascontiguousarray(w_in[:, :, np.concatenate([kr_cols, kg_cols, qg_cols])])
    w_uq = A(mla_w_uq)
    uq_cols = np.concatenate([np.concatenate([h * 96 + np.arange(64), h * 96 + 64 + p32]) for h in range(4)])
    w_uqsw = np.ascontiguousarray(w_uq[:, :, uq_cols])
    pp = np.zeros((128, L, NPL), f32)
    fm = lambda v: np.asarray(v, f32).reshape(-1, 128).T
    for l in range(L):
        pp[:, l, 0:8] = fm(g_pre_mix[l]); pp[:, l, 8:16] = fm(g_post_mix[l])
        pp[:, l, 16:24] = fm(g_pre_ffn[l]); pp[:, l, 24:32] = fm(g_post_ffn[l])
        pp[:, l, 32:80] = fm(b_ada[l])
        pp[:, l, 80:82] = fm(mla_g_q[l]); pp[:, l, 82:83] = fm(mla_g_kv[l])
        gq = np.asarray(gqa_g_q[l], f32); gk = np.asarray(gqa_g_k[l], f32)
        pp[:, l, 83] = np.tile(gq, 2); pp[:, l, 84] = np.tile(gq[p64], 2)
        pp[:, l, 85] = np.tile(gk, 2); pp[:, l, 86] = np.tile(gk[p64], 2)
        cwl = np.asarray(ffn_conv_w[l], f32)
        for t in range(3):
            pp[:, l, 87 + t * 44:87 + (t + 1) * 44] = fm(cwl[t])
        pp[:, l, 219:263] = fm(ffn_conv_b[l])
    pp = np.ascontiguousarray(pp.reshape(128, L * NPL))
    tm = np.zeros((128, L, 192), f32)
    for l in range(L):
        tm[:, l, 0:128] = np.asarray(mla_g_kv[l], f32)[None, :]
        tm[:, l, 128:192] = np.asarray(gqa_g_k[l], f32)[None, :]
    tm = np.ascontiguousarray(tm.reshape(128, L * 192))
    dsh = np.zeros((31, 127), f32)
    for j in range(31):
        dsh[j, j + 48] = 1.0
    cols = np.arange(64)
    cs = np.clip(cols - 8, 0, 48)
    cm = ((cols[:, None] >= cs[None, :]) & (cols[:, None] < cs[None, :] + 16)).astype(f32)
    colmask = np.ascontiguousarray(np.concatenate([cm, cm], 0))
    rowsel = np.zeros((2, 128), f32); rowsel[0, 0:64] = 1; rowsel[1, 64:128] = 1
    rpbr = np.ascontiguousarray(np.transpose(np.asarray(na_rpb, f32)[:, :, ::-1, :], (0, 3, 1, 2)).reshape(L, 31, 90))
    nck = np.asarray(cache_na_k, f32); ncv = np.asarray(cache_na_v, f32)
    mck = np.asarray(cache_mla_ckv, f32); mkr = np.asarray(cache_mla_krope, f32)
    gck = np.asarray(cache_gqa_k, f32); gcv = np.asarray(cache_gqa_v, f32)
    LW = NLAYERS_RUN
    if BISECT:
        w_ada = np.zeros((1, 128, 128), f32); w_out = w_ada; ffn_w_up = w_ada; ffn_w_down = w_ada
    shared = dict(w_ada=A(w_ada[:LW]), w_in=A(w_in[:LW]), w_insw=A(w_insw[:LW]), w_uq=w_uq, w_uqsw=w_uqsw, w_ukv=A(mla_w_ukv), w_out=A(w_out[:LW]),
                  w_up=A(ffn_w_up[:LW]), w_down=A(ffn_w_down[:LW]), pp=pp, tm=tm, dsh=dsh, colmask=colmask, rowsel=rowsel, rpbr=rpbr)
    in_maps = []
    for core in range(8):
        b, rk = core // 4, core % 4
        xp = x_prompt[core * 4:(core + 1) * 4].reshape(NT, D)
        xs = x_sample[b, rk * NT:(rk + 1) * NT]
        xT = np.ascontiguousarray(np.concatenate([xp, xs], 0).T)
        condT = np.ascontiguousarray(np.stack([c_ctx, c[b]], 1))
        ctxk = np.concatenate([nck[b].reshape(L, 256, 384), mck[b], mkr[b], gck[b].reshape(L, 256, 128)], -1)
        ctxk = np.ascontiguousarray(np.transpose(ctxk, (0, 2, 1)))
        ctxv = np.ascontiguousarray(np.concatenate([ncv[b].reshape(L, 256, 384), gcv[b].reshape(L, 256, 128)], -1))
        r0 = rk * 16
        mr = np.zeros((2, 12, 16), f32)
        for jj in range(12):
            for half in range(2):
                kr = r0 + 2 * jj - 4 + half
                for lr in range(16):
                    r = r0 + lr
                    rs = min(max(r - 4, 0), 56)
                    vis = (0 <= kr < 64) and (rs <= kr < rs + 8)
                    mr[half, jj, lr] = 0.0 if vis else -30000.0
        m = dict(shared)
        m.update(xT=xT, condT=condT, ctxk=ctxk, ctxv=ctxv, rope=_rope_tables(core), maskrow=np.ascontiguousarray(mr.reshape(2, 192)))
        in_maps.append(m)
    if "nc" not in _NC_CACHE:
        _NC_CACHE["nc"] = build()
    res = run_bass_kernel_spmd(_NC_CACHE["nc"], in_maps, core_ids=list(range(8)))
    y_prompt = np.zeros((32, 256, D), f32)
    y_sample = np.zeros((2, 4096, D), f32)
    caches = np.zeros((32, L, 256, R), f32)
    for core in range(8):
        r = res.results[core]
        yT = np.asarray(r["yT"], f32)
        y_prompt[core * 4:(core + 1) * 4] = yT[:, 0:NT].T.reshape(4, 256, D)
        b, rk = core // 4, core % 4
        y_sample[b, rk * NT:(rk + 1) * NT] = yT[:, NT:].T
        cc = np.asarray(r["caches"], f32).reshape(L, 4, 256, R)
        caches[core * 4:(core + 1) * 4] = np.transpose(cc, (1, 0, 2, 3))
    new_na_k = np.ascontiguousarray(caches[..., 0:384]).reshape(32, L, 256, 6, 64)
    new_na_v = np.ascontiguousarray(caches[..., 384:768]).reshape(32, L, 256, 6, 64)
    new_ckv = np.ascontiguousarray(caches[..., 768:896])
    new_kr = np.ascontiguousarray(caches[..., 896:928])
    new_gk = np.ascontiguousarray(caches[..., 928:1056]).reshape(32, L, 256, 2, 64)
    new_gv = np.ascontiguousarray(caches[..., 1056:1184]).reshape(32, L, 256, 2, 64)
    return (y_prompt, y_sample, new_na_k, new_na_v, new_ckv, new_kr, new_gk, new_gv)
```
